# Optimizing a Trainium2 kernel written in Bass

```python
import math
import jax, jax.numpy as jnp
from jax import lax
import numpy as np

D_MODEL = 2048
BATCH = 16
SEQ = 256
DEPTH = 2
DEC_BATCH = 8
DEC_SEQ = 1024
PAST_LEN = 256

GRID_W = 64
MIX_W = D_MODEL
GROUP_W = MIX_W // 4
CONV_W = GROUP_W
CONV_K = 31
SSM_W = GROUP_W
SSM_HEAD_DIM = 64
SSM_HEADS = SSM_W // SSM_HEAD_DIM
SSM_GROUPS = 2
SSM_D_STATE = 128
SSM_CONV_K = 5
SSM_BC = SSM_GROUPS * SSM_D_STATE
SSM_CONV_DIM = SSM_W + 2 * SSM_BC
RET_W = GROUP_W
RET_HEAD_DIM = 128
RET_HEADS = RET_W // RET_HEAD_DIM
ATT_HEAD_DIM = 64
ATT_HEADS = GROUP_W // ATT_HEAD_DIM
ATT_KV_HEADS = 2
ATT_WINDOW = 128
ATT_BLOCK = 128
ROPE_THETA = 10000.0
CHUNK = 128
D_FF = 4 * D_MODEL
EPS = 1e-6
IN_SIZES = (2 * CONV_W, SSM_W, SSM_CONV_DIM, 2 * SSM_HEADS, RET_W, RET_W, RET_W, RET_W,
            ATT_HEADS * ATT_HEAD_DIM, ATT_KV_HEADS * ATT_HEAD_DIM, ATT_KV_HEADS * ATT_HEAD_DIM)
N_IN = 2 * CONV_W + SSM_W + SSM_CONV_DIM + 2 * SSM_HEADS + 4 * RET_W + ATT_HEADS * ATT_HEAD_DIM + 2 * ATT_KV_HEADS * ATT_HEAD_DIM

kernel_name = 'hybrid_flow_backbone_step'


def rmsnorm(x, g):
    x32 = x.astype(jnp.float32)
    y = x32 * lax.rsqrt(jnp.mean(x32 * x32, axis=-1, keepdims=True) + EPS)
    return (y * g.astype(jnp.float32)).astype(x.dtype)


def layernorm(x, g, b):
    x32 = x.astype(jnp.float32)
    xc = x32 - jnp.mean(x32, axis=-1, keepdims=True)
    var = jnp.mean(xc * xc, axis=-1, keepdims=True)
    return (xc * lax.rsqrt(var + EPS) * g.astype(jnp.float32) + b.astype(jnp.float32)).astype(x.dtype)


def dwconv(x, w, b):
    pad = w.shape[0] // 2
    y = lax.conv_general_dilated(x, w[:, None, :].astype(x.dtype), (1,), [(pad, pad)],
                                 dimension_numbers=('NWC', 'WIO', 'NWC'),
                                 feature_group_count=x.shape[-1])
    return y + b.astype(x.dtype)


def conv_module(u, w_dw, b_dw, ln_g, ln_b):
    a, g = jnp.split(u, 2, axis=-1)
    h = dwconv(a * jax.nn.sigmoid(g), w_dw, b_dw)
    return jax.nn.silu(layernorm(h, ln_g, ln_b))


def chunked_recurrence(q, k, v, log_a, h0):
    bsz, L, H, N = q.shape
    P = v.shape[-1]
    nc = L // CHUNK
    f32 = jnp.float32
    qc = q.astype(f32).reshape(bsz, nc, CHUNK, H, N)
    kc = k.astype(f32).reshape(bsz, nc, CHUNK, H, N)
    vc = v.astype(f32).reshape(bsz, nc, CHUNK, H, P)
    a_cum = jnp.cumsum(log_a.astype(f32).reshape(bsz, nc, CHUNK, H), axis=2)
    a_t = jnp.moveaxis(a_cum, 3, 2)
    causal = jnp.tril(jnp.ones((CHUNK, CHUNK), dtype=bool))
    decay = jnp.exp(jnp.where(causal, a_t[..., :, None] - a_t[..., None, :], -jnp.inf))
    scores = jnp.einsum('bcqhn,bckhn->bchqk', qc, kc) * decay
    y_intra = jnp.einsum('bchqk,bckhp->bcqhp', scores, vc)
    a_last = a_cum[:, :, -1, :]
    w_end = jnp.exp(a_last[:, :, None, :] - a_cum)
    chunk_states = jnp.einsum('bckhn,bckh,bckhp->bchpn', kc, w_end, vc)

    def step(h, xs):
        s_c, a_c = xs
        return h * jnp.exp(a_c)[..., None, None] + s_c, h

    h_final, h_prev = lax.scan(step, h0.astype(f32),
                               (jnp.moveaxis(chunk_states, 1, 0), jnp.moveaxis(a_last, 1, 0)))
    h_prev = jnp.moveaxis(h_prev, 0, 1)
    y_inter = jnp.einsum('bcqhn,bchpn,bcqh->bcqhp', qc, h_prev, jnp.exp(a_cum))
    y = (y_intra + y_inter).reshape(bsz, L, H, P)
    return y.astype(v.dtype), h_final


def directional(q, k, v, log_a, h0, reverse):
    if reverse:
        q, k, v, log_a = (jnp.flip(t, axis=1) for t in (q, k, v, log_a))
    y, h = chunked_recurrence(q, k, v, log_a, h0)
    if reverse:
        y = jnp.flip(y, axis=1)
    return y, h


def ssm_mixer(z, xbc, dt_raw, conv_w, conv_b, a_log, dt_bias, d_skip, norm_g, h0):
    bsz, L, _ = z.shape
    xbc = jax.nn.silu(dwconv(xbc, conv_w, conv_b))
    xs, bm, cm = jnp.split(xbc, [SSM_W, SSM_W + SSM_BC], axis=-1)
    xs = xs.reshape(bsz, L, SSM_HEADS, SSM_HEAD_DIM)
    rep = SSM_HEADS // SSM_GROUPS
    bm = jnp.repeat(bm.reshape(bsz, L, SSM_GROUPS, SSM_D_STATE), rep, axis=2)
    cm = jnp.repeat(cm.reshape(bsz, L, SSM_GROUPS, SSM_D_STATE), rep, axis=2)
    dt = jax.nn.softplus(dt_raw.astype(jnp.float32).reshape(bsz, L, 2, SSM_HEADS)
                         + dt_bias.astype(jnp.float32))
    a_neg = -jnp.exp(a_log.astype(jnp.float32))
    y = 0.0
    finals = []
    for d in range(2):
        dt_d = dt[:, :, d]
        y_d, h_d = directional(cm, bm * dt_d[..., None], xs, dt_d * a_neg[d], h0[:, d], d == 1)
        y = y + y_d + d_skip[d][:, None] * xs
        finals.append(h_d)
    y = y.reshape(bsz, L, SSM_W)
    out = rmsnorm(y * jax.nn.silu(z), norm_g)
    return out.astype(z.dtype), jnp.stack(finals, axis=1).astype(z.dtype)


def retention_mixer(q, k, v, g, log_decay, gn_g, h0):
    bsz, L, _ = q.shape
    shp = (bsz, L, RET_HEADS, RET_HEAD_DIM)
    qh = q.reshape(shp)
    kh = k.reshape(shp) * (RET_HEAD_DIM ** -0.5)
    vh = v.reshape(shp)
    y = 0.0
    finals = []
    for d in range(2):
        log_a = jnp.broadcast_to(log_decay[d].astype(jnp.float32), (bsz, L, RET_HEADS))
        y_d, h_d = directional(qh, kh, vh, log_a, h0[:, d], d == 1)
        y = y + y_d
        finals.append(h_d)
    y32 = y.astype(jnp.float32)
    yc = y32 - jnp.mean(y32, axis=-1, keepdims=True)
    var = jnp.mean(yc * yc, axis=-1, keepdims=True)
    yn = (yc * lax.rsqrt(var + EPS)).reshape(bsz, L, RET_W) * gn_g.astype(jnp.float32)
    out = jax.nn.silu(g.astype(jnp.float32)) * yn
    return out.astype(q.dtype), jnp.stack(finals, axis=1).astype(q.dtype)


def rope_axis(x, pos):
    half = x.shape[-1] // 2
    freqs = ROPE_THETA ** (-jnp.arange(half, dtype=jnp.float32) / half)
    ang = pos.astype(jnp.float32)[:, None] * freqs
    cos = jnp.cos(ang)[:, None, :]
    sin = jnp.sin(ang)[:, None, :]
    x32 = x.astype(jnp.float32)
    x1, x2 = x32[..., :half], x32[..., half:]
    return jnp.concatenate([x1 * cos - x2 * sin, x2 * cos + x1 * sin], axis=-1).astype(x.dtype)


def rope_2d(x):
    L = x.shape[1]
    n_rows = L // GRID_W
    rows = jnp.repeat(jnp.arange(n_rows), GRID_W)
    cols = jnp.tile(jnp.arange(GRID_W), n_rows)
    d = x.shape[-1] // 2
    return jnp.concatenate([rope_axis(x[..., :d], rows), rope_axis(x[..., d:], cols)], axis=-1)


def ctx_attention(q, k, v, sink):
    bsz, Lc, H, d = q.shape
    G = H // ATT_KV_HEADS
    nb = Lc // ATT_BLOCK
    scale = d ** -0.5
    qb = jnp.moveaxis(q.reshape(bsz, nb, ATT_BLOCK, ATT_KV_HEADS, G, d), 1, 0)
    k32 = k.astype(jnp.float32)
    v32 = v.astype(jnp.float32)
    sink_l = sink.astype(jnp.float32).reshape(1, ATT_KV_HEADS, G, 1, 1)

    def one_block(qi):
        s = jnp.einsum('bqkgd,bskd->bkgqs', qi.astype(jnp.float32), k32) * scale
        s_all = jnp.concatenate([s, jnp.broadcast_to(sink_l, s.shape[:-1] + (1,))], axis=-1)
        p = jax.nn.softmax(s_all, axis=-1)[..., :-1]
        return jnp.einsum('bkgqs,bskd->bqkgd', p, v32)

    o = lax.map(one_block, qb)
    return jnp.moveaxis(o, 0, 1).reshape(bsz, Lc, H * d).astype(q.dtype)


def latent_attention(q, k, v, k_ctx, v_ctx, sink):
    bsz, L, H, d = q.shape
    G = H // ATT_KV_HEADS
    nb = L // ATT_BLOCK
    Lc = k_ctx.shape[1]
    scale = d ** -0.5
    f32 = jnp.float32
    qb = q.astype(f32).reshape(bsz, nb, ATT_BLOCK, ATT_KV_HEADS, G, d)
    pad = ((0, 0), (ATT_BLOCK, ATT_BLOCK), (0, 0), (0, 0))

    def band(t):
        tp = jnp.pad(t.astype(f32), pad).reshape(bsz, nb + 2, ATT_BLOCK, ATT_KV_HEADS, d)
        return jnp.concatenate([tp[:, :-2], tp[:, 1:-1], tp[:, 2:]], axis=2)

    kb, vb = band(k), band(v)
    q_pos = jnp.arange(nb)[:, None] * ATT_BLOCK + jnp.arange(ATT_BLOCK)[None, :]
    k_pos = (jnp.arange(nb)[:, None] - 1) * ATT_BLOCK + jnp.arange(3 * ATT_BLOCK)[None, :]
    kp = k_pos[:, None, :]
    mask = (jnp.abs(q_pos[:, :, None] - kp) <= ATT_WINDOW) & (kp >= 0) & (kp < L)
    s_band = jnp.einsum('bnqkgd,bnskd->bnkgqs', qb, kb) * scale
    s_band = jnp.where(mask[None, :, None, None], s_band, -jnp.inf)
    s_ctx = jnp.einsum('bnqkgd,bskd->bnkgqs', qb, k_ctx.astype(f32)) * scale
    s_sink = jnp.broadcast_to(sink.astype(f32).reshape(1, 1, ATT_KV_HEADS, G, 1, 1),
                              s_band.shape[:-1] + (1,))
    p = jax.nn.softmax(jnp.concatenate([s_ctx, s_band, s_sink], axis=-1), axis=-1)
    o = (jnp.einsum('bnkgqs,bskd->bnqkgd', p[..., :Lc], v_ctx.astype(f32))
         + jnp.einsum('bnkgqs,bnskd->bnqkgd', p[..., Lc:Lc + 3 * ATT_BLOCK], vb))
    return o.reshape(bsz, L, H * d).astype(q.dtype)


def mixer(h, p, ctx):
    bsz, L, _ = h.shape
    u = h @ p['w_in']
    splits = [int(s) for s in np.cumsum(IN_SIZES)[:-1]]
    u_conv, z, xbc, dt_raw, rq, rk, rv, rg, aq, ak, av = jnp.split(u, splits, axis=-1)
    o_conv = conv_module(u_conv, p['conv_w'], p['conv_b'], p['conv_ln_g'], p['conv_ln_b'])
    if ctx is None:
        h_ssm0 = jnp.zeros((bsz, 2, SSM_HEADS, SSM_HEAD_DIM, SSM_D_STATE), jnp.float32)
        h_ret0 = jnp.zeros((bsz, 2, RET_HEADS, RET_HEAD_DIM, RET_HEAD_DIM), jnp.float32)
    else:
        k_c, v_c, h_ssm0, h_ret0 = ctx
    o_ssm, h_ssm = ssm_mixer(z, xbc, dt_raw, p['ssm_conv_w'], p['ssm_conv_b'], p['ssm_a_log'],
                             p['ssm_dt_bias'], p['ssm_d'], p['ssm_norm'], h_ssm0)
    o_ret, h_ret = retention_mixer(rq, rk, rv, rg, p['ret_log_decay'], p['ret_gn_g'], h_ret0)
    aq = aq.reshape(bsz, L, ATT_HEADS, ATT_HEAD_DIM)
    ak = ak.reshape(bsz, L, ATT_KV_HEADS, ATT_HEAD_DIM)
    av = av.reshape(bsz, L, ATT_KV_HEADS, ATT_HEAD_DIM)
    if ctx is None:
        o_att = ctx_attention(aq, ak, av, p['att_sink'])
    else:
        o_att = latent_attention(rope_2d(aq), rope_2d(ak), av, k_c, v_c, p['att_sink'])
    out = jnp.concatenate([o_conv, o_ssm, o_ret, o_att], axis=-1) @ p['w_out']
    return out, (ak, av, h_ssm, h_ret)


def block(x, c_vec, p, ctx):
    mod = jax.nn.silu(c_vec) @ p['ada_w'] + p['ada_b']
    sh1, sc1, g1, sh2, sc2, g2 = jnp.split(mod[:, None, :], 6, axis=-1)
    h = rmsnorm(x, p['norm_mix']) * (1.0 + sc1) + sh1
    mix_out, states = mixer(h, p, ctx)
    x = x + g1 * mix_out
    h = rmsnorm(x, p['norm_mlp']) * (1.0 + sc2) + sh2
    x = x + g2 * (jnp.square(jax.nn.relu(h @ p['w1'])) @ p['w2'])
    return x, states


def setup_inputs(seed: int = 0) -> dict:
    key = jax.random.key(seed)
    ks = jax.random.split(key, 32)
    f32 = jnp.float32

    def nrm(k, shape, s):
        return s * jax.random.normal(k, shape, f32)

    D = D_MODEL
    dt_init = jnp.exp(jax.random.uniform(ks[20], (DEPTH, 2, SSM_HEADS), f32, math.log(1e-3), math.log(1e-1)))
    ret_base = jnp.log1p(-jnp.exp2(-5.0 - jnp.arange(RET_HEADS, dtype=f32)))
    return {
        'x_prompt': nrm(ks[0], (BATCH, SEQ, D), 1.0),
        'x_sample': nrm(ks[1], (DEC_BATCH, DEC_SEQ, D), 1.0),
        'cache_attn_k': nrm(ks[2], (DEC_BATCH, DEPTH, PAST_LEN, ATT_KV_HEADS, ATT_HEAD_DIM), 1.0),
        'cache_attn_v': nrm(ks[3], (DEC_BATCH, DEPTH, PAST_LEN, ATT_KV_HEADS, ATT_HEAD_DIM), 1.0),
        'state_ssm': nrm(ks[4], (DEC_BATCH, DEPTH, 2, SSM_HEADS, SSM_HEAD_DIM, SSM_D_STATE), 0.5),
        'state_ret': nrm(ks[5], (DEC_BATCH, DEPTH, 2, RET_HEADS, RET_HEAD_DIM, RET_HEAD_DIM), 1.0),
        'c': nrm(ks[6], (DEC_BATCH, D), 1.0),
        'c_ctx': nrm(ks[7], (D,), 1.0),
        'ada_w': nrm(ks[8], (DEPTH, D, 6 * D), 0.5 * D ** -0.5),
        'ada_b': nrm(ks[9], (DEPTH, 6 * D), 0.01),
        'norm_mix': 1.0 + nrm(ks[10], (DEPTH, D), 0.02),
        'norm_mlp': 1.0 + nrm(ks[11], (DEPTH, D), 0.02),
        'w_in': nrm(ks[12], (DEPTH, D, N_IN), D ** -0.5),
        'conv_w': nrm(ks[13], (DEPTH, CONV_K, CONV_W), CONV_K ** -0.5),
        'conv_b': nrm(ks[14], (DEPTH, CONV_W), 0.01),
        'conv_ln_g': 1.0 + nrm(ks[15], (DEPTH, CONV_W), 0.02),
        'conv_ln_b': nrm(ks[16], (DEPTH, CONV_W), 0.01),
        'ssm_conv_w': nrm(ks[17], (DEPTH, SSM_CONV_K, SSM_CONV_DIM), SSM_CONV_K ** -0.5),
        'ssm_conv_b': nrm(ks[18], (DEPTH, SSM_CONV_DIM), 0.01),
        'ssm_a_log': jnp.log(jax.random.uniform(ks[19], (DEPTH, 2, SSM_HEADS), f32, 1.0, 16.0)),
        'ssm_dt_bias': dt_init + jnp.log(-jnp.expm1(-dt_init)),
        'ssm_d': 1.0 + nrm(ks[21], (DEPTH, 2, SSM_HEADS), 0.1),
        'ssm_norm': 1.0 + nrm(ks[22], (DEPTH, SSM_W), 0.02),
        'ret_log_decay': ret_base * (1.0 + nrm(ks[23], (DEPTH, 2, RET_HEADS), 0.05)),
        'ret_gn_g': 1.0 + nrm(ks[24], (DEPTH, RET_W), 0.02),
        'att_sink': nrm(ks[25], (DEPTH, ATT_HEADS), 0.5),
        'w_out': nrm(ks[26], (DEPTH, MIX_W, D), MIX_W ** -0.5),
        'w1': nrm(ks[27], (DEPTH, D, D_FF), D ** -0.5),
        'w2': nrm(ks[28], (DEPTH, D_FF, D), D_FF ** -0.5),
        'final_norm': 1.0 + nrm(ks[29], (D,), 0.02),
    }


def reference(x_prompt, x_sample, cache_attn_k, cache_attn_v, state_ssm, state_ret, c, c_ctx,
              ada_w, ada_b, norm_mix, norm_mlp, w_in, conv_w, conv_b, conv_ln_g, conv_ln_b,
              ssm_conv_w, ssm_conv_b, ssm_a_log, ssm_dt_bias, ssm_d, ssm_norm,
              ret_log_decay, ret_gn_g, att_sink, w_out, w1, w2, final_norm):
    y_p = x_prompt
    y_s = x_sample
    new_k, new_v, new_ssm, new_ret = [], [], [], []
    for l in range(DEPTH):
        p = dict(ada_w=ada_w[l], ada_b=ada_b[l], norm_mix=norm_mix[l], norm_mlp=norm_mlp[l],
                 w_in=w_in[l], conv_w=conv_w[l], conv_b=conv_b[l], conv_ln_g=conv_ln_g[l],
                 conv_ln_b=conv_ln_b[l], ssm_conv_w=ssm_conv_w[l], ssm_conv_b=ssm_conv_b[l],
                 ssm_a_log=ssm_a_log[l], ssm_dt_bias=ssm_dt_bias[l], ssm_d=ssm_d[l],
                 ssm_norm=ssm_norm[l], ret_log_decay=ret_log_decay[l], ret_gn_g=ret_gn_g[l],
                 att_sink=att_sink[l], w_out=w_out[l], w1=w1[l], w2=w2[l])
        y_p, (k_l, v_l, hs_l, hr_l) = block(y_p, c_ctx[None, :], p, None)
        new_k.append(k_l)
        new_v.append(v_l)
        new_ssm.append(hs_l)
        new_ret.append(hr_l)
        y_s, _ = block(y_s, c, p, (cache_attn_k[:, l], cache_attn_v[:, l], state_ssm[:, l], state_ret[:, l]))
    y_prompt = rmsnorm(y_p, final_norm)
    y_sample = rmsnorm(y_s, final_norm)
    return (y_prompt, y_sample, jnp.stack(new_k, axis=1), jnp.stack(new_v, axis=1),
            jnp.stack(new_ssm, axis=1), jnp.stack(new_ret, axis=1))
```

```python
import numpy as np
import concourse.bass as bass
import concourse.mybir as mybir
from contextlib import ExitStack

F32 = mybir.dt.float32
BF16 = mybir.dt.bfloat16
I32 = mybir.dt.int32
AF = mybir.ActivationFunctionType
ALU = mybir.AluOpType
AX = mybir.AxisListType


class Region:
    __slots__ = ("w", "rs")

    def __init__(self):
        self.w = None
        self.rs = []


class T:
    def __init__(self, h, nreg=1, name="", excl=False, shape=None):
        self.h = h
        self.excl = excl
        self.shape = shape
        self.regs = [Region() for _ in range(nreg)]
        self.name = name

    def __getitem__(self, idx):
        return self.h[idx]


class R:
    __slots__ = ("ap", "t", "regs")

    def __init__(self, t, regs, ap):
        self.t = t
        if regs is None:
            regs = range(len(t.regs))
        elif isinstance(regs, int):
            regs = (regs,)
        self.regs = regs
        self.ap = ap


class Op:
    __slots__ = ("eng", "fn", "deps", "needs_inc", "sem", "sem_val", "is_dma", "id")

    def __init__(self, eng, fn, is_dma=False):
        self.eng = eng
        self.fn = fn
        self.deps = []
        self.needs_inc = False
        self.sem = None
        self.sem_val = None
        self.is_dma = is_dma


class K:
    ENG = ("pe", "act", "dve", "pool", "sp")

    def __init__(self, nc, es):
        self.nc = nc
        self.es = es
        self.ops = []
        self.engobj = {"pe": nc.tensor, "act": nc.scalar, "dve": nc.vector, "pool": nc.gpsimd, "sp": nc.sync}
        self.esem = {e: es.enter_context(nc.semaphore("sem_" + e)) for e in self.ENG}
        self.dsem = {}
        self.nalloc = 0
        self._bar_from = 0

    def sb(self, name, shape, dt, nreg=1):
        h = self.es.enter_context(self.nc.sbuf_tensor(name, list(shape), dt))
        return T(h, nreg, name, shape=list(shape))

    def ps(self, name, shape, dt, nreg=1):
        h = self.es.enter_context(self.nc.psum_tensor(name, list(shape), dt))
        return T(h, nreg, name, excl=True, shape=list(shape))

    def _track(self, op, reads, writes):
        deps = {}
        for r in reads:
            for i in r.regs:
                reg = r.t.regs[i]
                if reg.w is not None:
                    deps[id(reg.w)] = (reg.w, "raw")
                if r.t.excl:
                    for o in reg.rs:
                        if o.eng != op.eng:
                            deps[id(o)] = (o, "raw")
        for r in writes:
            for i in r.regs:
                reg = r.t.regs[i]
                if reg.w is not None and id(reg.w) not in deps:
                    deps[id(reg.w)] = (reg.w, "waw")
                for o in reg.rs:
                    if id(o) not in deps:
                        deps[id(o)] = (o, "war")
        for r in reads:
            for i in r.regs:
                reg = r.t.regs[i]
                if op.is_dma:
                    reg.rs.append(op)
                else:
                    reg.rs = [o for o in reg.rs if o.is_dma or o.eng != op.eng] + [op]
        for r in writes:
            for i in r.regs:
                reg = r.t.regs[i]
                reg.w = op
                reg.rs = []
        for o, kind in deps.values():
            if o is op:
                continue
            if (not o.is_dma) and (not op.is_dma) and o.eng == op.eng:
                if op.eng == "pe" or kind != "raw":
                    continue
            op.deps.append(o)
            o.needs_inc = True
        self.ops.append(op)

    def op(self, eng, fn, reads, writes):
        o = Op(eng, fn)
        self._track(o, reads, writes)
        return o

    def dma(self, queue, out, in_, semkey=None):
        eng = self.engobj[queue]
        oa, ia = out.ap, in_.ap
        o = Op(queue, lambda: eng.dma_start(out=oa, in_=ia), is_dma=True)
        if semkey is None:
            semkey = id(out.t)
        if semkey not in self.dsem:
            self.dsem[semkey] = [self.es.enter_context(self.nc.semaphore("dsem%d" % len(self.dsem))), 0]
        ent = self.dsem[semkey]
        ent[1] += 16
        o.sem = ent[0]
        o.sem_val = ent[1]
        self._track(o, [in_], [out])
        return o

    def mm(self, out, lhsT, rhs, start=True, stop=True, sgc=False):
        oa, la, ra = out.ap, lhsT.ap, rhs.ap
        nc = self.nc
        if sgc:
            return self.op("pe", lambda: nc.tensor.matmul(oa, la, ra, start=start, stop=stop, skip_group_check=True),
                           [lhsT, rhs], [out])
        return self.op("pe", lambda: nc.tensor.matmul(oa, la, ra, start=start, stop=stop),
                       [lhsT, rhs], [out])

    def tr(self, out, in_, ident):
        oa, ia, da = out.ap, in_.ap, ident.ap
        nc = self.nc
        return self.op("pe", lambda: nc.tensor.transpose(oa, ia, da), [in_, ident], [out])

    def _e(self, eng):
        return self.engobj[eng]

    def act(self, out, in_, func, bias=None, scale=None, accum_out=None, eng="act"):
        oa, ia = out.ap, in_.ap
        kw = {}
        reads = [in_]
        writes = [out]
        if bias is not None:
            if isinstance(bias, R):
                kw["bias"] = bias.ap
                reads.append(bias)
            else:
                kw["bias"] = bias
        if scale is not None:
            if isinstance(scale, R):
                kw["scale"] = scale.ap
                reads.append(scale)
            else:
                kw["scale"] = scale
        if accum_out is not None:
            kw["accum_out"] = accum_out.ap
            writes.append(accum_out)
        nc = self.nc
        return self.op("act", lambda: nc.scalar.activation(oa, ia, func, **kw), reads, writes)

    def tt(self, eng, out, in0, in1, op):
        oa, a, b = out.ap, in0.ap, in1.ap
        e = self._e(eng)
        return self.op(eng, lambda: e.tensor_tensor(oa, a, b, op), [in0, in1], [out])

    def ts(self, eng, out, in0, s1, op0, s2=None, op1=None, accum_out=None):
        oa, a = out.ap, in0.ap
        reads = [in0]
        writes = [out]
        if isinstance(s1, R):
            reads.append(s1)
            s1 = s1.ap
        if isinstance(s2, R):
            reads.append(s2)
            s2 = s2.ap
        kw = {}
        if op1 is not None:
            kw["op1"] = op1
        if accum_out is not None:
            kw["accum_out"] = accum_out.ap
            writes.append(accum_out)
        e = self._e(eng)
        return self.op(eng, lambda: e.tensor_scalar(oa, a, s1, s2, op0, **kw), reads, writes)

    def stt(self, eng, out, in0, scalar, in1, op0, op1):
        oa, a, b = out.ap, in0.ap, in1.ap
        reads = [in0, in1]
        if isinstance(scalar, R):
            reads.append(scalar)
            scalar = scalar.ap
        e = self._e(eng)
        return self.op(eng, lambda: e.scalar_tensor_tensor(oa, a, scalar, b, op0, op1), reads, [out])

    def copy(self, eng, out, in_):
        oa, ia = out.ap, in_.ap
        if eng == "act":
            nc = self.nc
            return self.op("act", lambda: nc.scalar.copy(oa, ia), [in_], [out])
        e = self._e(eng)
        return self.op(eng, lambda: e.tensor_copy(oa, ia), [in_], [out])

    def memset(self, eng, out, val):
        oa = out.ap
        e = self._e(eng)
        return self.op(eng, lambda: e.memset(oa, val), [], [out])

    def reduce(self, eng, out, in_, op, axis=AX.X):
        oa, ia = out.ap, in_.ap
        e = self._e(eng)
        return self.op(eng, lambda: e.tensor_reduce(oa, ia, axis, op), [in_], [out])

    def recip(self, out, in_):
        oa, ia = out.ap, in_.ap
        nc = self.nc
        return self.op("dve", lambda: nc.vector.reciprocal(oa, ia), [in_], [out])

    def barrier(self):
        last = {}
        dmas = []
        for o in self.ops[self._bar_from:]:
            if o.fn is None:
                continue
            if o.is_dma:
                dmas.append(o)
            else:
                last[o.eng] = o
        self._bar_from = len(self.ops)
        for e in ("pe", "act", "dve", "pool", "sp"):
            b = Op(e, None)
            for e2, o in last.items():
                if e2 != e:
                    b.deps.append(o)
                    o.needs_inc = True
            b.deps.extend(dmas)
            self.ops.append(b)

    def emit(self, final_wait_eng="sp"):
        cnt = {e: 0 for e in self.ENG}
        waited = {e: {} for e in self.ENG}
        nwait = 0
        for op in self.ops:
            eng = self.engobj[op.eng]
            w = {}
            for d in op.deps:
                assert d.sem_val is not None, "dep emitted after consumer"
                k = id(d.sem)
                if k not in w or w[k][1] < d.sem_val:
                    w[k] = (d.sem, d.sem_val)
            wd = waited[op.eng]
            for k, (sem, val) in w.items():
                if wd.get(k, 0) < val:
                    eng.wait_ge(sem, val)
                    wd[k] = val
                    nwait += 1
            if op.fn is None:
                op.sem_val = 0
                continue
            ins = op.fn()
            if op.is_dma:
                ins.then_inc(op.sem, 16)
            else:
                if op.needs_inc:
                    cnt[op.eng] += 1
                    op.sem = self.esem[op.eng]
                    op.sem_val = cnt[op.eng]
                    ins.then_inc(op.sem, 1)
                else:
                    op.sem = self.esem[op.eng]
                    op.sem_val = cnt[op.eng] + 0
            op.fn = None
        eng = self.engobj[final_wait_eng]
        for key, (sem, val) in self.dsem.items():
            eng.wait_ge(sem, val)
        for e in self.ENG:
            if cnt[e] > 0:
                eng.wait_ge(self.esem[e], cnt[e])
        self.stats = dict(nops=len(self.ops), nwait=nwait, cnt=cnt, ndsem=len(self.dsem))

import math

NL = 2
DM = 2048
EPS = 1e-6
NEG = -30000.0
OFF = dict(a=0, g=512, z=1024, xbc=1536, dt=2560, rq=2576, rk=3088, rv=3600, rg=4112, aq=4624, akv=5136)
CO = {}
_o = 0
for _n, _w in [("ident", 128), ("ones", 128), ("Lf", 128), ("Uf", 128), ("NEGf", 128), ("Lb", 128), ("Ub", 128),
               ("NEGb", 128), ("DIFFf", 128), ("DIFFb", 128), ("PERM", 128),
               ("colf", 1), ("colb", 1), ("pad0", 6), ("COS", 1024), ("SIN", 1024)]:
    CO[_n] = (_o, _w)
    _o += _w
NCST = _o
RO = {}
_o = 0
for _n, _w in [("a_log", 16), ("dt_bias", 16), ("ssm_d", 16), ("ret_ld", 8), ("sink", 8), ("ssm_norm", 512), ("gn_g", 512)]:
    RO[_n] = (_o, _w)
    _o += _w
NROW = _o


def make_consts():
    c = np.zeros((128, NCST), np.float32)
    p = np.arange(128)[:, None]
    f = np.arange(128)[None, :]

    def put(n, v):
        o, w = CO[n]
        c[:, o:o + w] = v
    put("ident", (p == f))
    put("ones", 1.0)
    put("Lf", (p > f))
    put("Uf", (p <= f))
    put("NEGf", NEG * (p > f))
    put("Lb", (p < f))
    put("Ub", (p >= f))
    put("NEGb", NEG * (p < f))
    put("DIFFf", np.maximum(f - p, 0))
    put("DIFFb", np.maximum(p - f, 0))
    d = np.arange(128) % 64
    i = d % 32
    partner = np.where(i < 16, np.arange(128) + 16, np.arange(128) - 16)
    perm = np.zeros((128, 128), np.float32)
    perm[partner, np.arange(128)] = 1.0
    put("PERM", perm)
    t = np.arange(1024)
    pos = np.where((d // 32 == 0)[:, None], (t // 64)[None, :], (t % 64)[None, :]).astype(np.float32)
    freqs = (10000.0 ** (-(np.arange(16, dtype=np.float32)) / 16.0)).astype(np.float32)
    ang = pos * freqs[i % 16][:, None]
    put("COS", np.cos(ang))
    put("SIN", np.sin(ang) * np.where(i < 16, -1.0, 1.0)[:, None])
    put("colf", (np.arange(128) + 1.0)[:, None])
    put("colb", (128.0 - np.arange(128))[:, None])
    return c


def bc(ap, dims):
    return bass.AP(ap.tensor, ap.offset, [list(ap.ap[0])] + [list(d) for d in dims])


class _WV:
    def __init__(self, buf, n):
        self.h = buf.h.rearrange("p (k n) -> p k n", n=n)
        self.regs = buf.regs
        self.excl = False
        self.name = buf.name

    def __getitem__(self, idx):
        return self.h[idx]


class _WVS:
    def __init__(self, t):
        self.h = t.h.rearrange("p (c n) -> p c n", n=128)
        self.regs = t.regs
        self.excl = False
        self.name = t.name

    def __getitem__(self, idx):
        return self.h[idx]


def run_pipeline(gens, depth):
    active = []
    src = iter(gens)
    done = False
    while True:
        if not done and len(active) < depth:
            g_ = next(src, None)
            if g_ is None:
                done = True
            else:
                active.append(g_)
        if not active:
            if done:
                break
            continue
        for g_ in list(active):
            try:
                next(g_)
            except StopIteration:
                active.remove(g_)


class Arena:
    def __init__(self, k, words):
        self.k = k
        self.h = k.es.enter_context(k.nc.sbuf_tensor("arena", [128, words], F32))
        self.off = 0
        self.words = words
        self.peak = 0

    def alloc(self, name, shape, dt, nreg=1):
        n = 1
        for s in shape[1:]:
            n *= s
        nw = n if dt == F32 else (n + 1) // 2
        nw = (nw + 7) // 8 * 8
        assert self.off + nw <= self.words, ("arena overflow", name, self.off, nw, self.words)
        v = self.h[:, self.off:self.off + nw]
        if dt != F32:
            v = v.bitcast(dt)
        v = v[:, 0:n]
        if len(shape) > 2:
            names = ["d%d" % i for i in range(len(shape) - 1)]
            pat = "p (" + " ".join(names) + ") -> p " + " ".join(names)
            kw = {names[i]: shape[i + 1] for i in range(1, len(names))}
            v = v.rearrange(pat, **kw)
        v = v[0:shape[0]]
        self.off += nw
        self.peak = max(self.peak, self.off)
        return T(v, nreg, name, shape=list(shape))


class Builder:
    def __init__(self, nc, es, dbg=None, plan=None):
        self.dry = plan is None
        self.wplan = plan
        self.nc = nc
        self.k = K(nc, es)
        self.dbg = dbg
        k = self.k
        self.D0 = T(None, 1)
        self.ar = Arena(k, 52800)
        self.B = [k.ps("bank%d" % i, [128, 512], F32) for i in range(8)]
        self.ada_banks = (0, 4)
        self.rr = 0

    def D(self, ap):
        return R(self.D0, (), ap)

    def dram_in(self, name, shape):
        return self.nc.dram_tensor(name, list(shape), F32, kind="ExternalInput").ap()

    def dram_out(self, name, shape):
        return self.nc.dram_tensor(name, list(shape), F32, kind="ExternalOutput").ap()

    def bank(self, lo=0, hi=4):
        i = lo + self.rr % (hi - lo)
        self.rr += 1
        return self.B[i]

    def build(self):
        k, nc, ar = self.k, self.nc, self.ar
        A = ar.alloc
        D = self.D
        xin = self.dram_in("xin", [1536, DM])
        cT_d = self.dram_in("cT", [128, 16, 2])
        ada_w = self.dram_in("ada_w", [NL, DM, 6 * DM])
        ada_bT = self.dram_in("ada_bT", [128, NL, 96])
        nmixT = self.dram_in("nmixT", [128, NL, 16])
        nmlpT = self.dram_in("nmlpT", [128, NL, 16])
        fnT = self.dram_in("fnT", [128, 16])
        w_in = self.dram_in("w_in", [NL, DM, 5392])
        w_out = self.dram_in("w_out", [NL, DM, DM])
        w1 = self.dram_in("w1", [NL, DM, 4 * DM])
        w2 = self.dram_in("w2", [NL, 4 * DM, DM])
        convp = self.dram_in("convp", [128, NL, 4, 34])
        sconvp = self.dram_in("sconvp", [128, NL, 8, 6])
        rowp = self.dram_in("rowp", [NL, NROW])
        cache_k = self.dram_in("cache_k", [NL, 256, 128])
        cache_v = self.dram_in("cache_v", [NL, 256, 128])
        st_ssm = self.dram_in("st_ssm", [NL, 2, 512, 128])
        st_ret = self.dram_in("st_ret", [NL, 2, 512, 128])
        cst_d = self.dram_in("cst", [128, NCST])
        y_out = self.dram_out("y", [1536, DM])
        nk_out = self.dram_out("newk", [2, NL, 256, 128])
        nv_out = self.dram_out("newv", [2, NL, 256, 128])
        ns_out = self.dram_out("newssm", [2, NL, 2, 512, 128])
        nr_out = self.dram_out("newret", [2, NL, 2, 512, 128])
        self.w_in, self.w_out, self.w1, self.w2, self.ada_w = w_in, w_out, w1, w2, ada_w
        if self.dbg:
            self.dbg_o = self.dram_out("dbg_o", [2, 4, 128, 4, 1024])
            self.dbg_x = self.dram_out("dbg_x", [2, 2, 128, 16, 1024])

        NSM = CO["COS"][0]
        self.cst_d = cst_d
        cst = A("cst", [128, NSM], F32)
        k.dma("sp", R(cst, None, cst[:]), D(cst_d[:, 0:NSM]))
        self.cst = cst

        def C(n, lo=0, hi=None):
            o, w = CO[n]
            hi = w if hi is None else hi
            return R(cst, None, cst[:, o + lo:o + hi])
        self.C = C
        cbf = A("cbf", [128, 4, 128], BF16)
        for i, n in enumerate(["ident", "ones", "NEGf", "NEGb"]):
            k.copy("dve", R(cbf, None, cbf[:, i, :]), C(n))
        self.identb = R(cbf, None, cbf[:, 0, :])
        self.onesb = R(cbf, None, cbf[:, 1, :])
        self.negfb = R(cbf, None, cbf[:, 2, :])
        self.negbb = R(cbf, None, cbf[:, 3, :])

        prm = A("prm", [128, NL, 16 * 2 + 16], F32)
        k.dma("sp", R(prm, None, prm[:, :, 0:16]), D(nmixT))
        k.dma("sp", R(prm, None, prm[:, :, 16:32]), D(nmlpT))
        k.dma("sp", R(prm, None, prm[:, 0, 32:48]), D(fnT))
        cvp = A("cvp", [128, NL, 4, 34], F32)
        k.dma("sp", R(cvp, None, cvp[:]), D(convp))
        scp = A("scp", [128, NL, 8, 6], F32)
        k.dma("sp", R(scp, None, scp[:]), D(sconvp))
        rowb = A("rowb", [128, NL, 64], F32)
        self.rowp = rowp
        for l in range(NL):
            k.dma("sp", R(rowb, None, rowb[:, l, :]), D(bass.AP(rowp.tensor, l * NROW, [[0, 128], [1, 64]])))
        self.cvp, self.scp, self.rowb, self.prm = cvp, scp, rowb, prm
        abT = A("abT", [128, NL, 96], F32)
        k.dma("sp", R(abT, None, abT[:]), D(ada_bT))
        modT = A("modT", [128, NL, 96, 2], F32)
        self.modT = modT
        dsc = A("dsc", [128, NL, 2, 16, 2], F32)
        self.dsc = dsc

        self.NWB = 2
        self.wb = [A("wbuf%d" % i, [128, 4096], BF16) for i in range(self.NWB)]
        if self.dry:
            self.wplan = []
            self.jend0 = 0
        else:
            self.jend0 = min(j for j, sp in enumerate(self.wplan) if sp[0][0] != "ada" and sp[0][1] == 1)
        self.wb4 = list(self.wb)
        self.wi = 0
        self.wissued = 0

        cTs = A("cTs", [128, 16, 2], F32)
        scT = A("scT", [128, 16, 2], BF16)
        k.dma("sp", R(cTs, None, cTs[:]), D(cT_d))
        k.act(R(scT, None, scT[:]), R(cTs, None, cTs[:]), AF.Silu)
        self.scT, self.abT = scT, abT
        _mr = A("mrow", [2, 256], F32)
        self.mrow = [_mr, _mr]
        self.ada_done = 0
        self.dsc_done = set()

        self.xT = A("xT", [128, 16, 1024], F32, nreg=16 * 2)
        self.hT = A("hT", [128, 16, 1024], BF16, nreg=2)
        self.outs = dict(y=y_out, nk=nk_out, nv=nv_out, ns=ns_out, nr=nr_out)
        self.ins = dict(xin=xin, ck=cache_k, cv=cache_v, ss=st_ssm, sr=st_ret)
        for g in range(2):
            mg = ar.off
            if g == 0:
                self.wb4 = list(self.wb) + [A("wbufx%d" % i, [128, 4096], BF16) for i in range(2)]
                self.ada_until(16)
            self.group(g)
            k.barrier()
            ar.off = mg
        assert self.wi == len(self.wplan), (self.wi, len(self.wplan))
        if self.dry:
            return
        k.emit()
        print("stats", k.stats, "arena peak words", ar.peak)

    def spec(self, tag):
        nm = tag[0]
        if nm == "ada":
            _, l, ti = tag
            return (tag, "ada_w", l, 0, ti * 256, 256)
        if nm == "wo":
            _, g, l, mi, hf = tag
            return (tag, "w_out", l, mi * 512, hf * 1024, 1024)
        if nm == "w1":
            _, g, l, q, i = tag
            return (tag, "w1", l, 0, q * 2048 + i * 256, 256)
        if nm == "w2":
            _, g, l, q, i = tag
            return (tag, "w2", l, q * 2048, i * 256, 256)
        _, g, l, i = tag
        c0 = dict(g=512, a=0, xbc=1536, dt=2560, z=1024, rq=2576, rk=3088, rv=3600, rg=4112, aq=4624, akv=5136)[nm]
        return (tag, "w_in", l, 0, c0 + i * 256, 16 if nm == "dt" else 256)

    def ada_tile(self, gi):
        k = self.k
        l, ti = gi // 48, gi % 48
        scT, abT, modT, cst = self.scT, self.abT, self.modT, self.cst
        wt = self.wnext(("ada", l, ti))
        bk = self.bank(*self.ada_banks)
        for kc in range(16):
            k.mm(R(bk, None, bk[0:2, 0:256]), R(scT, None, scT[:, kc, :]), R(wt, None, wt[:, kc, :]),
                 start=(kc == 0), stop=(kc == 15))
        mr = self.mrow[gi % 2]
        k.copy("act", R(mr, None, mr[:]), R(bk, None, bk[0:2, 0:256]))
        bk2 = self.bank(*self.ada_banks)
        io = CO["ident"][0]
        for j in range(2):
            k.tr(R(bk2, None, bk2[:, j * 2:j * 2 + 2]), R(mr, None, mr[0:2, j * 128:(j + 1) * 128]),
                 R(cst, None, cst[0:2, io:io + 2]))
        ab = abT[:, l, 2 * ti:2 * ti + 2]
        k.tt("dve", R(modT, None, modT[:, l, 2 * ti:2 * ti + 2, :]),
             R(bk2, None, bk2[:, 0:4].rearrange("p (j r) -> p j r", r=2)),
             R(abT, None, bc(ab, [[1, 2], [0, 2]])), ALU.add)

    def ada_until(self, gi_end):
        while self.ada_done < min(gi_end, 96):
            self.ada_tile(self.ada_done)
            self.ada_done += 1

    def slot(self, n=1):
        self.ada_until(self.ada_done + n)

    def need_dsc(self, l, which):
        if (l, which) in self.dsc_done:
            return
        self.dsc_done.add((l, which))
        self.ada_until(l * 48 + (16 if which == 0 else 40))
        k, dsc, modT, prm = self.k, self.dsc, self.modT, self.prm
        pj, sj = [(0, 1), (1, 4)][which]
        pp = prm[:, l, pj * 16:(pj + 1) * 16]
        k.stt("dve", R(dsc, None, dsc[:, l, which, :, :]), R(modT, None, modT[:, l, sj * 16:(sj + 1) * 16, :]), 1.0,
              R(prm, None, bc(pp, [[1, 16], [0, 2]])), ALU.add, ALU.mult)

    def wbuf_of(self, j):
        if (not self.dry) and j < self.jend0 - 4:
            return self.wb4[j % 4]
        return self.wb[j % 2]

    def wissue(self, j):
        tag, wn, l, r0, c0, nc_ = self.wplan[j]
        w = getattr(self, wn)
        buf = self.wbuf_of(j)
        if tag[0] == "wo":
            src = w[l, r0:r0 + 512, c0:c0 + nc_].rearrange("(kc p) n -> p kc n", p=128)
            self.k.dma("pool", R(buf, None, buf[:].rearrange("p (k n) -> p k n", n=1024)), self.D(src))
        else:
            src = w[l, r0:r0 + 2048, c0:c0 + nc_].rearrange("(kc p) n -> p kc n", p=128)
            self.k.dma("pool", R(buf, None, buf[:].rearrange("p (k n) -> p k n", n=256)[:, :, 0:nc_]), self.D(src))

    def wnext(self, tag):
        j = self.wi
        if self.dry:
            self.wplan.append(self.spec(tag))
        else:
            assert self.wplan[j][0] == tag, (self.wplan[j][0], tag)
            jt = self.jend0 - 4
            while self.wissued < len(self.wplan) and self.wissued < j + (4 if self.wissued < jt else 2):
                self.wissue(self.wissued)
                self.wissued += 1
        self.wi += 1
        buf = self.wbuf_of(j)
        n = 1024 if tag[0] == "wo" else 256
        return _WV(buf, n)

    def projA(self, wt, mlist, T_, consumer, hsrc=None, slot=True, tick=None):
        k = self.k
        hT = self.hT if hsrc is None else hsrc
        NT = T_ // 512
        for m in mlist:
            bks = [self.bank(0, 4) for _ in range(NT)]
            for kc in range(16):
                for nt in range(NT):
                    k.mm(R(bks[nt], None, bks[nt][:]), R(wt, None, wt[:, kc, m * 128:(m + 1) * 128]),
                         R(hT, nt if hsrc is None else None, hT[:, kc, nt * 512:(nt + 1) * 512]),
                         start=(kc == 0), stop=(kc == 15))
            for nt in range(NT):
                consumer(m, nt, bks[nt])
            if tick is not None:
                tick()
        if slot:
            self.slot()

    def projB(self, wt, T_, c0, ncols, consumer, tick=None):
        k = self.k
        hT = self.hT
        for tt in range(T_ // 128):
            bk = self.bank(0, 4)
            for kc in range(16):
                k.mm(R(bk, None, bk[:, 0:ncols]), R(hT, tt // 4, hT[:, kc, tt * 128:(tt + 1) * 128]),
                     R(wt, None, wt[:, kc, c0:c0 + ncols]), start=(kc == 0), stop=(kc == 15))
            consumer(tt, bk)
            if tick is not None:
                tick()
        self.slot()

    def norm(self, g, T_, scale_fn, bias_fn, out_fn, sqt):
        k = self.k
        xT = self.xT
        NT = T_ // 512
        for nt in range(NT):
            ss = self.B[4 + nt % 2]
            for c in range(16):
                sq = sqt[c % 2]
                k.act(R(sq, None, sq[:]), R(xT, c * 2 + nt, xT[:, c, nt * 512:(nt + 1) * 512]), AF.Square)
                k.mm(R(ss, None, ss[:]), self.onesb, R(sq, None, sq[:]), start=(c == 0), stop=(c == 15))
            rstd = sqt[2]
            k.act(R(rstd, None, rstd[:]), R(ss, None, ss[:]), AF.Sqrt, bias=EPS, scale=1.0 / DM)
            k.recip(R(rstd, None, rstd[:]), R(rstd, None, rstd[:]))
            for c in range(16):
                tmp = sqt[3 + c % 2]
                k.stt("dve", R(tmp, None, tmp[:]), R(xT, c * 2 + nt, xT[:, c, nt * 512:(nt + 1) * 512]), scale_fn(c),
                      R(rstd, None, rstd[:]), ALU.mult, ALU.mult)
                b = bias_fn(c)
                if b is None:
                    k.copy("act", out_fn(c, nt), R(tmp, None, tmp[:]))
                else:
                    k.act(out_fn(c, nt), R(tmp, None, tmp[:]), AF.Identity, bias=b, scale=1.0)

    def norm_to_h(self, g, l, which, T_):
        self.need_dsc(l, which)
        ar = self.ar
        m0 = ar.off
        sqt = [ar.alloc("nsq%d" % i, [128, 512], BF16) for i in range(2)] + \
              [ar.alloc("nrs", [128, 512], F32)] + [ar.alloc("ntmp%d" % i, [128, 512], F32) for i in range(2)]
        r = g
        dsc, modT, hT = self.dsc, self.modT, self.hT
        shj = 0 if which == 0 else 3
        self.norm(g, T_,
                  lambda c: R(dsc, None, dsc[:, l, which, c, r:r + 1]),
                  lambda c: R(modT, None, modT[:, l, shj * 16 + c, r:r + 1]),
                  lambda c, nt: R(hT, nt, hT[:, c, nt * 512:(nt + 1) * 512]), sqt)
        self.k.barrier()
        ar.off = m0

    def group(self, g):
        k, ar = self.k, self.ar
        T_ = 512 if g == 0 else 1024
        tok0 = 0 if g == 0 else 512
        NTT = T_ // 128
        xT = self.xT
        xin = self.ins["xin"]
        identf = self.C("ident")
        m0 = ar.off
        xtok = [ar.alloc("xtok%d" % i, [128, DM], F32) for i in range(2)]
        for tt in range(NTT):
            xt = xtok[tt % 2]
            k.dma("sp", R(xt, None, xt[:]), self.D(xin[tok0 + tt * 128: tok0 + (tt + 1) * 128, :]))
            for q in range(4):
                bk = self.bank(0, 4)
                for j in range(4):
                    c = q * 4 + j
                    k.tr(R(bk, None, bk[:, j * 128:(j + 1) * 128]), R(xt, None, xt[:, c * 128:(c + 1) * 128]), identf)
                regs = [(q * 4 + j) * 2 + tt // 4 for j in range(4)]
                k.copy("act" if q % 2 else "dve", R(xT, regs, xT[:, q * 4:(q + 1) * 4, tt * 128:(tt + 1) * 128]),
                       R(bk, None, bk[:].rearrange("p (j t) -> p j t", t=128)))
        k.barrier()
        ar.off = m0
        for l in range(NL):
            self.layer(g, l, T_)
        m0 = ar.off
        sqt = [ar.alloc("nsq%d" % i, [128, 512], BF16) for i in range(2)] + \
              [ar.alloc("nrs", [128, 512], F32)] + [ar.alloc("ntmp%d" % i, [128, 512], F32) for i in range(2)]
        prm = self.prm
        self.norm(g, T_, lambda c: R(prm, None, prm[:, 0, 32 + c:33 + c]), lambda c: None,
                  lambda c, nt: R(xT, c * 2 + nt, xT[:, c, nt * 512:(nt + 1) * 512]), sqt)
        ytok = [ar.alloc("ytok%d" % i, [128, DM], F32) for i in range(2)]
        y_out = self.outs["y"]
        for tt in range(NTT):
            yt = ytok[tt % 2]
            for q in range(4):
                bk = self.bank(0, 4)
                for j in range(4):
                    c = q * 4 + j
                    k.tr(R(bk, None, bk[:, j * 128:(j + 1) * 128]),
                         R(xT, c * 2 + tt // 4, xT[:, c, tt * 128:(tt + 1) * 128]), identf)
                k.copy("act" if q % 2 else "dve", R(yt, None, yt[:, q * 512:(q + 1) * 512]), R(bk, None, bk[:]))
            k.dma("sp", self.D(y_out[tok0 + tt * 128: tok0 + (tt + 1) * 128, :]), R(yt, None, yt[:]), semkey="yout")
        k.barrier()
        ar.off = m0

    def layer(self, g, l, T_):
        k, ar = self.k, self.ar
        xT, modT = self.xT, self.modT
        NT = T_ // 512
        r = g
        self.norm_to_h(g, l, 0, T_)
        for mi, fn in enumerate((self.mix_conv, self.mix_ssm, self.mix_ret, self.mix_att)):
            m0 = ar.off
            self.oTm = ar.alloc("oTm", [128, 4, T_], BF16)
            oT = self.oTm
            fn(g, l, T_)
            self.ada_until(l * 48 + 24)
            if self.dbg and l == 0:
                dst_ = ar.alloc("dbgst", [128, 256], F32)
                for c_ in range(4):
                    for pc in range(T_ // 256):
                        k.copy("dve", R(dst_, None, dst_[:]), R(oT, None, oT[:, c_, pc * 256:(pc + 1) * 256]))
                        k.dma("sp", self.D(self.dbg_o[g, mi, :, c_, pc * 256:(pc + 1) * 256]), R(dst_, None, dst_[:]), semkey="dbgo")
            for hf in range(2):
                wv = self.wnext(("wo", g, l, mi, hf))
                for m in range(8):
                    c = hf * 8 + m
                    for nt in range(NT):
                        bk = self.bank(0, 4)
                        for kc in range(4):
                            k.mm(R(bk, None, bk[:]), R(wv, None, wv[:, kc, m * 128:(m + 1) * 128]),
                                 R(oT, None, oT[:, kc, nt * 512:(nt + 1) * 512]), start=(kc == 0), stop=(kc == 3))
                        xr = R(xT, c * 2 + nt, xT[:, c, nt * 512:(nt + 1) * 512])
                        k.stt("dve", xr, R(bk, None, bk[:]), R(modT, None, modT[:, l, 2 * 16 + c, r:r + 1]), xr, ALU.mult, ALU.add)
            k.barrier()
            ar.off = m0
        if self.dbg and l == 0:
            k.dma("sp", self.D(self.dbg_x[g, 0, :, :, 0:T_]), R(xT, None, xT[:, :, 0:T_]), semkey="dbgx")
        self.norm_to_h(g, l, 1, T_)
        m0 = ar.off
        self.ada_until(l * 48 + 48)
        hid = ar.alloc("hid", [128, 16, T_], BF16)
        relu_t = [ar.alloc("relu%d" % i, [128, 512], F32) for i in range(2)]
        for q in range(4):
            for i in range(8):
                wt = self.wnext(("w1", g, l, q, i))

                def cons1(m, nt, bk, i=i):
                    c = i * 2 + m
                    rl = relu_t[(c + nt) % 2]
                    k.act(R(rl, None, rl[:]), R(bk, None, bk[:]), AF.Relu)
                    k.tt("pool", R(hid, None, hid[:, c, nt * 512:(nt + 1) * 512]), R(rl, None, rl[:]), R(rl, None, rl[:]), ALU.mult)
                self.projA(wt, [0, 1], T_, cons1)
            for i in range(8):
                wt = self.wnext(("w2", g, l, q, i))

                def cons2(m, nt, bk, i=i):
                    c = i * 2 + m
                    xr = R(xT, c * 2 + nt, xT[:, c, nt * 512:(nt + 1) * 512])
                    k.stt("dve", xr, R(bk, None, bk[:]), R(modT, None, modT[:, l, 5 * 16 + c, r:r + 1]), xr, ALU.mult, ALU.add)
                self.projA(wt, [0, 1], T_, cons2, hsrc=hid)
        k.barrier()
        ar.off = m0
        if self.dbg and l == 0:
            k.dma("sp", self.D(self.dbg_x[g, 1, :, :, 0:T_]), R(xT, None, xT[:, :, 0:T_]), semkey="dbgx")

    def seqtiles(self, g):
        if g == 0:
            return [(0, 0, 256), (1, 256, 256)]
        return [(0, 0, 512), (0, 512, 512)]

    def dwconv_pe(self, diag, K_, pad_t, L, s, t0, n, bk):
        k = self.k
        loc = t0 - s * L
        for kk in range(K_):
            k.mm(R(bk, None, bk[:, 0:n]), R(diag, None, diag[:, kk, :]), R(pad_t, None, pad_t[:, s, loc + kk: loc + kk + n]),
                 start=(kk == 0), stop=(kk == K_ - 1))

    def mix_conv(self, g, l, T_):
        k, ar = self.k, self.ar
        A = ar.alloc
        nseq, L = (2, 256) if g == 0 else (1, 1024)
        cvp, oT = self.cvp, self.oTm
        cbf = self.identb
        sg = A("sg", [128, 4, T_], BF16)
        glu = [A("glu%d" % c, [128, nseq, L + 30], BF16) for c in range(4)]
        for c in range(4):
            k.memset("pool", R(glu[c], None, glu[c][:]), 0.0)
        for i in range(2):
            wt = self.wnext(("g", g, l, i))

            def cons(m, nt, bk, i=i):
                c = i * 2 + m
                k.act(R(sg, None, sg[:, c, nt * 512:(nt + 1) * 512]), R(bk, None, bk[:]), AF.Sigmoid)
            self.projA(wt, [0, 1], T_, cons)
        for i in range(2):
            wt = self.wnext(("a", g, l, i))

            def cons(m, nt, bk, i=i):
                c = i * 2 + m
                for (s, t0, n) in self.seqtiles(g):
                    if t0 // 512 != nt:
                        continue
                    lo = t0 - nt * 512
                    loc = t0 - s * L
                    k.tt("dve", R(glu[c], None, glu[c][:, s, 15 + loc:15 + loc + n]), R(bk, None, bk[:, lo:lo + n]),
                         R(sg, None, sg[:, c, t0:t0 + n]), ALU.mult)
            self.projA(wt, [0, 1], T_, cons)
        hc = A("hc", [128, 4, 512], F32)
        hb = [A("hb%d" % i, [128, 512], BF16) for i in range(2)]
        sq = [A("csq%d" % i, [128, 512], BF16) for i in range(2)]
        diag = [A("cdiag%d" % i, [128, 31, 128], BF16) for i in range(2)]
        st = [A("cst%d" % i, [128, 512], F32) for i in range(4)]
        tmp = [A("ctmp%d" % i, [128, 512], F32) for i in range(2)]
        idb = self.identb.ap
        for pi, (s, t0, n) in enumerate(self.seqtiles(g)):
            S1, S2 = self.B[4], self.B[5]
            for c in range(4):
                dg = diag[c % 2]
                w = cvp[:, l, c, 0:31]
                k.tt("dve", R(dg, None, dg[:]), R(self.identb.t, None, bc(idb, [[0, 31], [1, 128]])),
                     R(cvp, None, bc(w, [[1, 31], [0, 128]])), ALU.mult)
                bk = self.bank(0, 4)
                self.dwconv_pe(dg, 31, glu[c], L, s, t0, n, bk)
                bias = R(cvp, None, cvp[:, l, c, 31:32])
                k.act(R(hc, None, hc[:, c, 0:n]), R(bk, None, bk[:, 0:n]), AF.Identity, bias=bias, scale=1.0)
                k.act(R(sq[c % 2], None, sq[c % 2][:, 0:n]), R(bk, None, bk[:, 0:n]), AF.Square, bias=bias, scale=1.0)
                k.copy("dve", R(hb[c % 2], None, hb[c % 2][:, 0:n]), R(hc, None, hc[:, c, 0:n]))
                k.mm(R(S1, None, S1[:, 0:n]), self.onesb, R(hb[c % 2], None, hb[c % 2][:, 0:n]), start=(c == 0), stop=(c == 3))
                k.mm(R(S2, None, S2[:, 0:n]), self.onesb, R(sq[c % 2], None, sq[c % 2][:, 0:n]), start=(c == 0), stop=(c == 3))
            mean, m2, var, rstd = st
            k.ts("dve", R(mean, None, mean[:, 0:n]), R(S1, None, S1[:, 0:n]), 1.0 / 512, ALU.mult)
            k.tt("dve", R(m2, None, m2[:, 0:n]), R(mean, None, mean[:, 0:n]), R(mean, None, mean[:, 0:n]), ALU.mult)
            k.stt("dve", R(var, None, var[:, 0:n]), R(S2, None, S2[:, 0:n]), 1.0 / 512, R(m2, None, m2[:, 0:n]), ALU.mult, ALU.subtract)
            k.act(R(rstd, None, rstd[:, 0:n]), R(var, None, var[:, 0:n]), AF.Sqrt, bias=EPS, scale=1.0)
            k.recip(R(rstd, None, rstd[:, 0:n]), R(rstd, None, rstd[:, 0:n]))
            for c in range(4):
                tp = tmp[c % 2]
                k.tt("dve", R(tp, None, tp[:, 0:n]), R(hc, None, hc[:, c, 0:n]), R(mean, None, mean[:, 0:n]), ALU.subtract)
                k.tt("dve", R(tp, None, tp[:, 0:n]), R(tp, None, tp[:, 0:n]), R(rstd, None, rstd[:, 0:n]), ALU.mult)
                k.act(R(oT, None, oT[:, c, t0:t0 + n]), R(tp, None, tp[:, 0:n]), AF.Silu,
                      bias=R(cvp, None, cvp[:, l, c, 33:34]), scale=R(cvp, None, cvp[:, l, c, 32:33]))

    def load_state(self, dst, src_ap, stg):
        k, ar = self.k, self.ar
        k.dma("sp", R(stg, None, stg[:]), self.D(src_ap.rearrange("(c p) n -> p c n", p=128)))
        bk = self.bank(0, 4)
        for c in range(4):
            k.tr(R(bk, None, bk[:, c * 128:(c + 1) * 128]), R(stg, None, stg[:, c, :]), self.C("ident"))
        k.copy("dve", R(dst, None, dst[:]), R(bk, None, bk[:]))

    def store_state(self, src, dst_ap, stg, key, bank=None):
        k = self.k
        bk = self.bank(0, 4) if bank is None else bank
        for c in range(4):
            k.tr(R(bk, None, bk[:, c * 128:(c + 1) * 128]), R(src, None, src[:, c * 128:(c + 1) * 128]), self.C("ident"))
        k.copy("dve", R(stg, None, stg[:].rearrange("p c n -> p (c n)")), R(bk, None, bk[:]))
        k.dma("sp", self.D(dst_ap.rearrange("(c p) n -> p c n", p=128)), R(stg, None, stg[:]), semkey=key)

    def mix_ssm(self, g, l, T_):
        k, ar = self.k, self.ar
        A = ar.alloc
        nseq, L = (2, 256) if g == 0 else (1, 1024)
        CH = L // 128
        NTT = T_ // 128
        scp, rowb, oT = self.scp, self.rowb, self.oTm
        idb = self.identb.ap
        identf = self.C("ident")
        BCT = A("BCT", [128, 4, T_], BF16)
        xs_tok = A("xs_tok", [128, NTT, 512], BF16)
        B_tok = A("B_tok", [128, NTT, 256], BF16)
        zs = A("zs", [128, NTT, 512], BF16)
        Hst = A("Hst", [128, NTT, 512], BF16)
        la = A("la", [128, NTT, 16], F32)
        lndt = A("lndt", [128, NTT, 16], F32)
        dsum = A("dsum", [128, 8], F32)
        dsk = A("dsk", [128, 8, 128], BF16)
        ngt = A("ngt", [128, 512], F32)
        Hf = A("Hf", [128, 512], F32)
        Hb = A("Hb", [128, 512], F32)
        m1 = ar.off
        pad = [A("spad%d" % i, [128, nseq, L + 4], BF16) for i in range(2)]
        for p_ in pad:
            k.memset("pool", R(p_, None, p_[:]), 0.0)
        ftmp = [A("sft%d" % i, [128, 512], BF16) for i in range(2)]
        dg5 = [A("dg5_%d" % i, [128, 5, 128], BF16) for i in range(2)]
        for i in range(4):
            wt = self.wnext(("xbc", g, l, i))

            def cons(m, nt, bk, i=i):
                pd = pad[m]
                for (s, t0, n) in self.seqtiles(g):
                    if t0 // 512 != nt:
                        continue
                    lo = t0 - nt * 512
                    loc = t0 - s * L
                    k.copy("act", R(pd, None, pd[:, s, 2 + loc:2 + loc + n]), R(bk, None, bk[:, lo:lo + n]))
            self.projA(wt, [0, 1], T_, cons)
            for m in range(2):
                c = i * 2 + m
                dg = dg5[m]
                w = scp[:, l, c, 0:5]
                k.tt("dve", R(dg, None, dg[:]), R(self.identb.t, None, bc(idb, [[0, 5], [1, 128]])),
                     R(scp, None, bc(w, [[1, 5], [0, 128]])), ALU.mult)
                bias = R(scp, None, scp[:, l, c, 5:6])
                for (s, t0, n) in self.seqtiles(g):
                    bk = self.bank(0, 4)
                    self.dwconv_pe(dg, 5, pad[m], L, s, t0, n, bk)
                    if c < 4:
                        ft = ftmp[(c + t0 // 256) % 2]
                        k.act(R(ft, None, ft[:, 0:n]), R(bk, None, bk[:, 0:n]), AF.Silu, bias=bias, scale=1.0)
                        bk2 = self.B[6]
                        b2 = bk2[:].bitcast(BF16)
                        for j in range(n // 128):
                            k.tr(R(bk2, None, b2[:, j * 128:(j + 1) * 128]), R(ft, None, ft[:, j * 128:(j + 1) * 128]), self.identb)
                        tt0 = t0 // 128
                        k.copy("dve", R(xs_tok, None, xs_tok[:, tt0:tt0 + n // 128, c * 128:(c + 1) * 128]),
                               R(bk2, None, b2[:, 0:n].rearrange("p (j t) -> p j t", t=128)))
                    else:
                        k.act(R(BCT, None, BCT[:, c - 4, t0:t0 + n]), R(bk, None, bk[:, 0:n]), AF.Silu, bias=bias, scale=1.0)
                        if c < 6:
                            bk2 = self.B[6]
                            b2 = bk2[:].bitcast(BF16)
                            for j in range(n // 128):
                                k.tr(R(bk2, None, b2[:, j * 128:(j + 1) * 128]),
                                     R(BCT, None, BCT[:, c - 4, t0 + j * 128:t0 + (j + 1) * 128]), self.identb)
                            tt0 = t0 // 128
                            k.copy("dve", R(B_tok, None, B_tok[:, tt0:tt0 + n // 128, (c - 4) * 128:(c - 3) * 128]),
                                   R(bk2, None, b2[:, 0:n].rearrange("p (j t) -> p j t", t=128)))
        wt = self.wnext(("dt", g, l, 0))
        aneg = A("aneg", [128, 16], F32)
        ro = RO["a_log"][0]
        k.act(R(aneg, None, aneg[:]), R(rowb, None, rowb[:, l, ro:ro + 16]), AF.Exp)
        dtb = RO["dt_bias"][0]
        dtt = A("dtt", [128, 16], F32)

        def cons(tt, bk):
            k.tt("dve", R(dtt, None, dtt[:]), R(bk, None, bk[:, 0:16]), R(rowb, None, rowb[:, l, dtb:dtb + 16]), ALU.add)
            k.act(R(dtt, None, dtt[:]), R(dtt, None, dtt[:]), AF.Exp)
            k.act(R(dtt, None, dtt[:]), R(dtt, None, dtt[:]), AF.Ln, bias=1.0, scale=1.0)
            k.act(R(lndt, None, lndt[:, tt, :]), R(dtt, None, dtt[:]), AF.Ln)
            k.stt("dve", R(la, None, la[:, tt, :]), R(dtt, None, dtt[:]), -1.0, R(aneg, None, aneg[:]), ALU.mult, ALU.mult)
        self.projB(wt, T_, 0, 16, cons)
        do = RO["ssm_d"][0]
        k.tt("dve", R(dsum, None, dsum[:]), R(rowb, None, rowb[:, l, do:do + 8]), R(rowb, None, rowb[:, l, do + 8:do + 16]), ALU.add)
        k.tt("dve", R(dsk, None, dsk[:]), R(self.identb.t, None, bc(idb, [[0, 8], [1, 128]])),
             R(dsum, None, bc(dsum[:], [[1, 8], [0, 128]])), ALU.mult)
        k.dma("sp", R(ngt, None, ngt[:]), self.D(bass.AP(self.rowp.tensor, l * NROW + RO["ssm_norm"][0], [[0, 128], [1, 512]])))
        SM = [[A("ssm_p1sm%d" % i, [128, 16], F32) for i in range(3)]]
        XS = [A("xsw_p1", [128, 512], BF16)]
        STG = [_WVS(A("stg_p1", [128, 512], F32))]
        ones_f = self.C("ones")
        Lm = [self.C("Lf"), self.C("Lb")]
        Um = [self.C("Uf"), self.C("Ub")]
        Ng = [self.C("NEGf"), self.C("NEGb")]

        def state_step(d, tt, H, use_bank):
            ex, wd, et = SM[0][0], SM[0][1], SM[0][2]
            b = use_bank
            k.mm(R(b, None, b[:, 0:8]), Lm[d], R(la, None, la[:, tt, d * 8:(d + 1) * 8]))
            k.mm(R(b, None, b[:, 8:16]), ones_f, R(la, None, la[:, tt, d * 8:(d + 1) * 8]))
            k.tt("dve", R(wd, None, wd[:, 0:8]), R(b, None, b[:, 0:8]), R(lndt, None, lndt[:, tt, d * 8:(d + 1) * 8]), ALU.add)
            k.act(R(wd, None, wd[:, 0:8]), R(wd, None, wd[:, 0:8]), AF.Exp)
            k.act(R(et, None, et[:, 0:8]), R(b, None, b[:, 8:16]), AF.Exp)
            xw = XS[0]
            k.tt("dve", R(xw, None, xw[:].rearrange("p (h q) -> p h q", q=64)),
                 R(xs_tok, None, xs_tok[:, tt, :].rearrange("p (h q) -> p h q", q=64)),
                 R(wd, None, bc(wd[:, 0:8], [[1, 8], [0, 64]])), ALU.mult)
            for gg in range(2):
                k.mm(R(b, None, b[:, gg * 256:(gg + 1) * 256]), R(B_tok, None, B_tok[:, tt, gg * 128:(gg + 1) * 128]),
                     R(xw, None, xw[:, gg * 256:(gg + 1) * 256]))
            k.tt("dve", R(H, None, H[:].rearrange("p (h q) -> p h q", q=64)), R(H, None, H[:].rearrange("p (h q) -> p h q", q=64)),
                 R(et, None, bc(et[:, 0:8], [[1, 8], [0, 64]])), ALU.mult)
            k.tt("dve", R(H, None, H[:]), R(H, None, H[:]), R(b, None, b[:]), ALU.add)

        def pass1():
            for s in range(nseq):
                if g == 0:
                    k.memset("dve", R(Hb, None, Hb[:]), 0.0)
                else:
                    self.load_state(Hb, self.ins["ss"][l, 1], STG[0])
                for c in reversed(range(CH)):
                    tt = s * CH + c
                    k.copy("act", R(Hst, None, Hst[:, tt, :]), R(Hb, None, Hb[:]))
                    state_step(1, tt, Hb, self.B[7])
                    yield
                if g == 0:
                    self.store_state(Hb, self.outs["ns"][s, l, 1], STG[0], "nsout")
                    yield
        p1 = pass1()
        for i in range(2):
            wt = self.wnext(("z", g, l, i))

            def cons(tt, bk, i=i):
                k.act(R(zs, None, zs[:, tt, i * 256:(i + 1) * 256]), R(bk, None, bk[:, 0:256]), AF.Silu)
            self.projB(wt, T_, 0, 256, cons, tick=lambda: next(p1, None))
        for _ in p1:
            pass
        k.barrier()
        ar.off = m1
        Hfb = A("Hfb", [128, 512], BF16)
        sm = [A("ssm_sm%d" % i, [128, 16], F32) for i in range(4)]
        SM[0] = sm
        XS[0] = A("xsw", [128, 512], BF16)
        rhsA = [A("rhsA%d" % i, [128, 4, 128], F32) for i in range(2)]
        rhsB = [A("rhsB%d" % i, [128, 4, 128], F32) for i in range(2)]
        Ep = [A("Ep%d" % i, [128, 4, 128], F32) for i in range(2)]
        ST = [A("ST%d" % i, [128, 8, 128], BF16) for i in range(2)]
        yt = [A("syt%d" % i, [128, 512], F32) for i in range(3)]
        ot = A("sot", [128, 512], BF16)
        ststg = _WVS(yt[2])
        STG[0] = ststg
        for s in range(nseq):
            if g == 0:
                k.memset("dve", R(Hf, None, Hf[:]), 0.0)
            else:
                self.load_state(Hf, self.ins["ss"][l, 0], ststg)
            for c in range(CH):
                tt = s * CH + c
                tk = slice(tt * 128, (tt + 1) * 128)
                y0, y1, y2 = yt
                k.copy("act", R(Hfb, None, Hfb[:]), R(Hf, None, Hf[:]))
                eq = sm[3]
                sb_ = self.B[7]
                for d in range(2):
                    k.mm(R(sb_, None, sb_[:, 16 + d * 8:24 + d * 8]), Um[d], R(la, None, la[:, tt, d * 8:(d + 1) * 8]))
                k.act(R(eq, None, eq[:]), R(sb_, None, sb_[:, 16:32]), AF.Exp)
                Pf_, Pb_ = self.B[4], self.B[3]
                for gg in range(2):
                    k.mm(R(Pf_, None, Pf_[:, gg * 256:(gg + 1) * 256]), R(BCT, None, BCT[:, 2 + gg, tk]), R(Hfb, None, Hfb[:, gg * 256:(gg + 1) * 256]))
                for gg in range(2):
                    k.mm(R(Pb_, None, Pb_[:, gg * 256:(gg + 1) * 256]), R(BCT, None, BCT[:, 2 + gg, tk]), R(Hst, None, Hst[:, tt, gg * 256:(gg + 1) * 256]))
                state_step(0, tt, Hf, self.B[7])
                k.tt("dve", R(y0, None, y0[:].rearrange("p (h q) -> p h q", q=64)), R(Pf_, None, Pf_[:].rearrange("p (h q) -> p h q", q=64)),
                     R(eq, None, bc(eq[:, 0:8], [[1, 8], [0, 64]])), ALU.mult)
                k.tt("dve", R(y1, None, y1[:].rearrange("p (h q) -> p h q", q=64)), R(Pb_, None, Pb_[:].rearrange("p (h q) -> p h q", q=64)),
                     R(eq, None, bc(eq[:, 8:16], [[1, 8], [0, 64]])), ALU.mult)
                k.tt("dve", R(y0, None, y0[:]), R(y0, None, y0[:]), R(y1, None, y1[:]), ALU.add)
                Gb = self.B[6]
                for gg in range(2):
                    k.mm(R(Gb, None, Gb[:, gg * 128:(gg + 1) * 128]), R(BCT, None, BCT[:, gg, tk]), R(BCT, None, BCT[:, 2 + gg, tk]))
                Yb = self.B[5]
                first = True
                for d in range(2):
                    for hf in range(2):
                        hs = d * 8 + hf * 4
                        k.tt("pool", R(rhsA[hf], None, rhsA[hf][:]), R(self.cst, None, bc(Um[d].ap, [[0, 4], [1, 128]])),
                             R(la, None, bc(la[:, tt, hs:hs + 4], [[1, 4], [0, 128]])), ALU.mult)
                        k.tt("pool", R(rhsB[hf], None, rhsB[hf][:]), R(self.cst, None, bc(Ng[d].ap, [[0, 4], [1, 128]])),
                             R(lndt, None, bc(lndt[:, tt, hs:hs + 4], [[1, 4], [0, 128]])), ALU.add)
                        bk = self.bank(0, 3)
                        k.mm(R(bk, None, bk[:]), Lm[d], R(rhsA[hf], None, rhsA[hf][:].rearrange("p h q -> p (h q)")),
                             start=True, stop=False)
                        k.mm(R(bk, None, bk[:]), identf, R(rhsB[hf], None, rhsB[hf][:].rearrange("p h q -> p (h q)")),
                             start=False, stop=True)
                        k.act(R(Ep[hf], None, Ep[hf][:].rearrange("p h q -> p (h q)")), R(bk, None, bk[:]), AF.Exp)
                        k.tt("dve", R(ST[d], None, ST[d][:, hf * 4:(hf + 1) * 4, :]), R(Ep[hf], None, Ep[hf][:]),
                             R(Gb, None, bc(Gb[:, hf * 128:(hf + 1) * 128], [[0, 4], [1, 128]])), ALU.mult)
                    for h in range(8):
                        lhss = [R(ST[d], None, ST[d][:, h, :])]
                        if d == 0:
                            lhss.append(R(dsk, None, dsk[:, h, :]))
                        for lhs in lhss:
                            k.mm(R(Yb, None, Yb[:, h * 64:(h + 1) * 64]), lhs, R(xs_tok, None, xs_tok[:, tt, h * 64:(h + 1) * 64]),
                                 start=first, stop=False, sgc=True)
                            first = False
                k.tt("dve", R(y0, None, y0[:]), R(y0, None, y0[:]), R(Yb, None, Yb[:]), ALU.add)
                k.tt("dve", R(y0, None, y0[:]), R(y0, None, y0[:]), R(zs, None, zs[:, tt, :]), ALU.mult)
                ssq = sm[0]
                k.tt("dve", R(y2, None, y2[:]), R(y0, None, y0[:]), R(y0, None, y0[:]), ALU.mult)
                k.reduce("dve", R(sm[0], None, ssq[:, 8:9]), R(y2, None, y2[:]), ALU.add)
                k.act(R(sm[0], None, ssq[:, 9:10]), R(sm[0], None, ssq[:, 8:9]), AF.Ln, bias=EPS, scale=1.0 / 512)
                k.act(R(sm[0], None, ssq[:, 10:11]), R(sm[0], None, ssq[:, 9:10]), AF.Exp, scale=-0.5)
                k.stt("dve", R(ot, None, ot[:]), R(y0, None, y0[:]), R(sm[0], None, ssq[:, 10:11]),
                      R(ngt, None, ngt[:]), ALU.mult, ALU.mult)
                bk2 = self.B[6]
                b2 = bk2[:].bitcast(BF16)
                for j in range(4):
                    k.tr(R(bk2, None, b2[:, j * 128:(j + 1) * 128]), R(ot, None, ot[:, j * 128:(j + 1) * 128]), self.identb)
                k.copy("act", R(oT, None, oT[:, 0:4, tk]), R(bk2, None, b2[:, 0:512].rearrange("p (j t) -> p j t", t=128)))
                self.slot()
            if g == 0:
                self.store_state(Hf, self.outs["ns"][s, l, 0], ststg, "nsout")

    def mix_ret(self, g, l, T_):
        k, ar = self.k, self.ar
        A = ar.alloc
        nseq, L = (2, 256) if g == 0 else (1, 1024)
        CH = L // 128
        NTT = T_ // 128
        rowb, oT = self.rowb, self.oTm
        identf = self.C("ident")
        QT = A("QT", [128, 4, T_], BF16)
        KT = A("KT", [128, 4, T_], BF16)
        K_tok = A("K_tok", [128, NTT, 512], BF16)
        V_tok = A("V_tok", [128, NTT, 512], BF16)
        gs = A("gs", [128, NTT, 512], BF16)
        Hst = A("rHst", [128, NTT, 512], BF16)
        E = A("rE", [128, 2, 4, 128], F32)
        eq = A("req", [128, 8], F32)
        etot = A("retot", [128, 8], F32)
        Hf = A("rHf", [128, 512], F32)
        Hb = A("rHb", [128, 512], F32)
        m1 = ar.off
        gnt = A("gnt", [128, 512], F32)
        k.dma("sp", R(gnt, None, gnt[:]), self.D(bass.AP(self.rowp.tensor, l * NROW + RO["gn_g"][0], [[0, 128], [1, 512]])))
        etmp = A("retmp", [128, 128], F32)
        etmp4 = A("retmp4", [128, 512], F32)
        rgt = [A("rgt%d" % i, [128, 256], F32) for i in range(2)]
        for i in range(2):
            wt = self.wnext(("rk", g, l, i))

            def cons(m, nt, bk, i=i):
                c = i * 2 + m
                k.act(R(KT, None, KT[:, c, nt * 512:(nt + 1) * 512]), R(bk, None, bk[:]), AF.Copy, scale=128.0 ** -0.5)
                bk2 = self.B[6]
                b2 = bk2[:].bitcast(BF16)
                for j in range(4):
                    k.tr(R(bk2, None, b2[:, j * 128:(j + 1) * 128]),
                         R(KT, None, KT[:, c, nt * 512 + j * 128: nt * 512 + (j + 1) * 128]), self.identb)
                k.copy("dve", R(K_tok, None, K_tok[:, nt * 4:nt * 4 + 4, c * 128:(c + 1) * 128]),
                       R(bk2, None, b2[:, 0:512].rearrange("p (j t) -> p j t", t=128)))
            self.projA(wt, [0, 1], T_, cons)
        for i in range(2):
            wt = self.wnext(("rv", g, l, i))

            def cons(tt, bk, i=i):
                k.copy("act", R(V_tok, None, V_tok[:, tt, i * 256:(i + 1) * 256]), R(bk, None, bk[:, 0:256]))
            self.projB(wt, T_, 0, 256, cons)
        ldo = RO["ret_ld"][0]
        DIFF = [self.C("DIFFf"), self.C("DIFFb")]
        Ng = [self.C("NEGf"), self.C("NEGb")]
        for d in range(2):
            for h in range(4):
                k.stt("dve", R(etmp, None, etmp[:]), DIFF[d], R(rowb, None, rowb[:, l, ldo + d * 4 + h: ldo + d * 4 + h + 1]),
                      Ng[d], ALU.mult, ALU.add)
                k.act(R(E, None, E[:, d, h, :]), R(etmp, None, etmp[:]), AF.Exp)
        cf, cb_ = CO["colf"][0], CO["colb"][0]
        cst = self.cst
        for d, co in enumerate((cf, cb_)):
            k.ts("dve", R(eq, None, eq[:, d * 4:(d + 1) * 4]), R(rowb, None, rowb[:, l, ldo + d * 4: ldo + d * 4 + 4]),
                 R(cst, None, cst[:, co:co + 1]), ALU.mult)
        k.act(R(eq, None, eq[:]), R(eq, None, eq[:]), AF.Exp)
        k.act(R(etot, None, etot[:]), R(rowb, None, rowb[:, l, ldo:ldo + 8]), AF.Exp, scale=128.0)
        VW_ = [A("rVw1", [128, 512], BF16)]
        SB_ = self.B[7]
        def state_step(d, tt, H, b):
            col = 127 if d == 0 else 0
            wcol = E[:, d, 0, col:col + 1]
            vw = VW_[0]
            k.tt("dve", R(vw, None, vw[:].rearrange("p (h q) -> p h q", q=128)),
                 R(V_tok, None, V_tok[:, tt, :].rearrange("p (h q) -> p h q", q=128)),
                 R(E, None, bc(wcol, [[128, 4], [0, 128]])), ALU.mult)
            for h in range(4):
                k.mm(R(b, None, b[:, h * 128:(h + 1) * 128]), R(K_tok, None, K_tok[:, tt, h * 128:(h + 1) * 128]),
                     R(vw, None, vw[:, h * 128:(h + 1) * 128]))
            k.tt("dve", R(H, None, H[:].rearrange("p (h q) -> p h q", q=128)), R(H, None, H[:].rearrange("p (h q) -> p h q", q=128)),
                 R(etot, None, bc(etot[:, d * 4:(d + 1) * 4], [[1, 4], [0, 128]])), ALU.mult)
            k.tt("dve", R(H, None, H[:]), R(H, None, H[:]), R(b, None, b[:]), ALU.add)


        def pass1():
            for s in range(nseq):
                if g == 0:
                    k.memset("dve", R(Hb, None, Hb[:]), 0.0)
                else:
                    self.load_state(Hb, self.ins["sr"][l, 1], _WVS(etmp4))
                for c in reversed(range(CH)):
                    tt = s * CH + c
                    k.copy("act", R(Hst, None, Hst[:, tt, :]), R(Hb, None, Hb[:]))
                    state_step(1, tt, Hb, SB_)
                    yield
                if g == 0:
                    self.store_state(Hb, self.outs["nr"][s, l, 1], _WVS(etmp4), "nrout")
                    yield
        p1 = pass1()
        tick = lambda: next(p1, None)
        for i in range(2):
            wt = self.wnext(("rq", g, l, i))

            def cons(m, nt, bk, i=i):
                k.copy("act", R(QT, None, QT[:, i * 2 + m, nt * 512:(nt + 1) * 512]), R(bk, None, bk[:]))
            self.projA(wt, [0, 1], T_, cons, tick=tick)
        for i in range(2):
            wt = self.wnext(("rg", g, l, i))

            def cons(tt, bk, i=i):
                rt_ = rgt[tt % 2]
                k.act(R(rt_, None, rt_[:]), R(bk, None, bk[:, 0:256]), AF.Silu)
                k.tt("dve", R(gs, None, gs[:, tt, i * 256:(i + 1) * 256]), R(rt_, None, rt_[:]),
                     R(gnt, None, gnt[:, i * 256:(i + 1) * 256]), ALU.mult)
            self.projB(wt, T_, 0, 256, cons, tick=tick)
        for _ in p1:
            pass
        k.barrier()
        ar.off = m1
        P_ = 2
        Hfb = [A("rHfb%d" % i, [128, 512], BF16) for i in range(P_)]
        VW_[0] = A("rVw", [128, 512], BF16)
        ST = [[A("rST%d_%d" % (i, d), [128, 4, 128], BF16) for d in range(2)] for i in range(P_)]
        _y1 = A("ryt_1", [128, 512], F32)
        _y2 = A("ryt_2", [128, 512], F32)
        yts = [[A("ryt%d_0" % i, [128, 512], F32), _y1, _y2] for i in range(P_)]
        sms = [A("rsm%d" % i, [128, 32], F32) for i in range(P_)]
        ots = [A("rot%d" % i, [128, 512], BF16) for i in range(P_)]
        ststg = _WVS(yts[0][2])
        GB_, YB_, PF_, PB_, TB_ = self.B[6], self.B[5], self.B[4], self.B[3], self.B[2]
        old_banks = self.ada_banks
        self.ada_banks = (0, 2)

        def chunk(it, s, c):
            tt = s * CH + c
            tk = slice(tt * 128, (tt + 1) * 128)
            hfb = Hfb[it % P_]
            k.copy("act", R(hfb, None, hfb[:]), R(Hf, None, Hf[:]))
            state_step(0, tt, Hf, SB_)
            if g == 0 and c == CH - 1:
                self.store_state(Hf, self.outs["nr"][s, l, 0], ststg, "nrout", bank=TB_)
            for h in range(4):
                k.mm(R(GB_, None, GB_[:, h * 128:(h + 1) * 128]), R(KT, None, KT[:, h, tk]), R(QT, None, QT[:, h, tk]))
            yield
            st = ST[it % P_]
            for d in range(2):
                k.tt("dve", R(st[d], None, st[d][:]), R(E, None, E[:, d, :, :]),
                     R(GB_, None, GB_[:].rearrange("p (h q) -> p h q", q=128)), ALU.mult)
            yield
            first = True
            for h in range(4):
                for d in range(2):
                    k.mm(R(YB_, None, YB_[:, h * 128:(h + 1) * 128]), R(st[d], None, st[d][:, h, :]),
                         R(V_tok, None, V_tok[:, tt, h * 128:(h + 1) * 128]), start=first, stop=False, sgc=True)
                    first = False
            for d, (Pb, Hsrc) in enumerate(((PF_, R(hfb, None, hfb[:])), (PB_, R(Hst, None, Hst[:, tt, :])))):
                for h in range(4):
                    k.mm(R(Pb, None, Pb[:, h * 128:(h + 1) * 128]), R(QT, None, QT[:, h, tk]),
                         R(Hsrc.t, None, Hsrc.ap[:, h * 128:(h + 1) * 128]))
            yield
            y0, y1, y2 = yts[it % P_]
            sm = sms[it % P_]
            for d, Pb in enumerate((PF_, PB_)):
                dst = y0 if d == 0 else y1
                k.tt("dve", R(dst, None, dst[:].rearrange("p (h q) -> p h q", q=128)),
                     R(Pb, None, Pb[:].rearrange("p (h q) -> p h q", q=128)),
                     R(eq, None, bc(eq[:, d * 4:(d + 1) * 4], [[1, 4], [0, 128]])), ALU.mult)
            k.tt("dve", R(y0, None, y0[:]), R(y0, None, y0[:]), R(YB_, None, YB_[:]), ALU.add)
            k.tt("pool", R(y0, None, y0[:]), R(y0, None, y0[:]), R(y1, None, y1[:]), ALU.add)
            yield
            y03 = y0[:].rearrange("p (h q) -> p h q", q=128)
            k.reduce("dve", R(sm, None, sm[:, 0:4]), R(y0, None, y03), ALU.add)
            k.tt("pool", R(y2, None, y2[:]), R(y0, None, y0[:]), R(y0, None, y0[:]), ALU.mult)
            k.reduce("dve", R(sm, None, sm[:, 4:8]), R(y2, None, y2[:].rearrange("p (h q) -> p h q", q=128)), ALU.add)
            k.ts("dve", R(sm, None, sm[:, 8:12]), R(sm, None, sm[:, 0:4]), 1.0 / 128, ALU.mult)
            k.tt("dve", R(sm, None, sm[:, 12:16]), R(sm, None, sm[:, 8:12]), R(sm, None, sm[:, 8:12]), ALU.mult)
            k.stt("dve", R(sm, None, sm[:, 16:20]), R(sm, None, sm[:, 4:8]), 1.0 / 128, R(sm, None, sm[:, 12:16]), ALU.mult, ALU.subtract)
            k.act(R(sm, None, sm[:, 24:28]), R(sm, None, sm[:, 16:20]), AF.Ln, bias=EPS, scale=1.0)
            k.act(R(sm, None, sm[:, 20:24]), R(sm, None, sm[:, 24:28]), AF.Exp, scale=-0.5)
            yield
            k.tt("dve", R(y0, None, y03), R(y0, None, y03), R(sm, None, bc(sm[:, 8:12], [[1, 4], [0, 128]])), ALU.subtract)
            k.tt("dve", R(y0, None, y03), R(y0, None, y03), R(sm, None, bc(sm[:, 20:24], [[1, 4], [0, 128]])), ALU.mult)
            ot = ots[it % P_]
            k.tt("pool", R(ot, None, ot[:]), R(y0, None, y0[:]), R(gs, None, gs[:, tt, :]), ALU.mult)
            yield
            b2 = TB_[:].bitcast(BF16)
            for j in range(4):
                k.tr(R(TB_, None, b2[:, j * 128:(j + 1) * 128]), R(ot, None, ot[:, j * 128:(j + 1) * 128]), self.identb)
            k.copy("act", R(oT, None, oT[:, 0:4, tk]), R(TB_, None, b2[:, 0:512].rearrange("p (j t) -> p j t", t=128)))
            self.slot()

        def chunks():
            it = 0
            for s in range(nseq):
                for c in range(CH):
                    yield chunk(it, s, c)
                    it += 1

        def chunk_wrapped(it, s, c):
            if c == 0:
                if g == 0:
                    k.memset("dve", R(Hf, None, Hf[:]), 0.0)
                else:
                    self.load_state(Hf, self.ins["sr"][l, 0], ststg)
            yield from chunk(it, s, c)

        def chunks2():
            it = 0
            for s in range(nseq):
                for c in range(CH):
                    yield chunk_wrapped(it, s, c)
                    it += 1
        run_pipeline(chunks2(), P_)
        self.ada_banks = old_banks

    def mix_att(self, g, l, T_):
        k, ar = self.k, self.ar
        A = ar.alloc
        nseq, L = (2, 256) if g == 0 else (1, 1024)
        NB = L // 128
        NTT = T_ // 128
        rowb, oT, cst = self.rowb, self.oTm, self.cst
        identf = self.C("ident")
        qT = A("qT", [128, 4, T_], BF16)
        kT = A("kT", [128, T_], BF16)
        kvtok = A("kvtok", [128, NTT, 256], F32)
        v_tok = A("v_tok", [128, NTT, 128], BF16)
        q32 = [A("q32_%d" % i, [128, 512], F32) for i in range(2)]
        rt = [A("rt%d" % i, [128, 512], F32) for i in range(2)]
        if g == 1:
            ropet = A("ropet", [128, 2048], F32)
            k.dma("sp", R(ropet, None, ropet[:]), self.D(self.cst_d[:, CO["COS"][0]:CO["COS"][0] + 2048]))

        def rope_or_copy(dst_ap_fn, bk, nt, idx):
            if g == 0:
                k.copy("act", R(dst_ap_fn[0], None, dst_ap_fn[1]), R(bk, None, bk[:]))
                return
            q = q32[idx % 2]
            k.copy("act", R(q, None, q[:]), R(bk, None, bk[:]))
            rb = self.B[6]
            k.mm(R(rb, None, rb[:]), self.C("PERM"), R(q, None, q[:]))
            t1, t2 = rt
            k.tt("dve", R(t1, None, t1[:]), R(q, None, q[:]), R(ropet, None, ropet[:, nt * 512:(nt + 1) * 512]), ALU.mult)
            k.tt("dve", R(t2, None, t2[:]), R(rb, None, rb[:]), R(ropet, None, ropet[:, 1024 + nt * 512:1024 + (nt + 1) * 512]), ALU.mult)
            k.tt("dve", R(dst_ap_fn[0], None, dst_ap_fn[1]), R(t1, None, t1[:]), R(t2, None, t2[:]), ALU.add)

        for i in range(2):
            wt = self.wnext(("aq", g, l, i))

            def cons(m, nt, bk, i=i):
                c = i * 2 + m
                rope_or_copy((qT, qT[:, c, nt * 512:(nt + 1) * 512]), bk, nt, c)
            self.projA(wt, [0, 1], T_, cons)
        wt = self.wnext(("akv", g, l, 0))

        def cons(m, nt, bk):
            rope_or_copy((kT, kT[:, nt * 512:(nt + 1) * 512]), bk, nt, 0)
        self.projA(wt, [0], T_, cons, slot=False)

        def consb(tt, bk):
            k.copy("act", R(kvtok, None, kvtok[:, tt, :]), R(bk, None, bk[:, 0:256]))
            k.copy("dve", R(v_tok, None, v_tok[:, tt, :]), R(kvtok, None, kvtok[:, tt, 128:256]))
        self.projB(wt, T_, 0, 256, consb)
        if g == 0:
            for s in range(2):
                k.dma("sp", self.D(self.outs["nk"][s, l].rearrange("(t p) d -> p t d", p=128)),
                      R(kvtok, None, kvtok[:, s * 2:s * 2 + 2, 0:128]), semkey="nkout")
                k.dma("sp", self.D(self.outs["nv"][s, l].rearrange("(t p) d -> p t d", p=128)),
                      R(kvtok, None, kvtok[:, s * 2:s * 2 + 2, 128:256]), semkey="nvout")
        if g == 1:
            ckv = A("ckv", [128, 2, 2, 128], F32)
            k.dma("sp", R(ckv, None, ckv[:, 0, :, :]), self.D(self.ins["ck"][l].rearrange("(t p) d -> p t d", p=128)))
            k.dma("sp", R(ckv, None, ckv[:, 1, :, :]), self.D(self.ins["cv"][l].rearrange("(t p) d -> p t d", p=128)))
            kcT = A("kcT", [128, 256], BF16)
            bkc = self.bank(0, 4)
            for t in range(2):
                k.tr(R(bkc, None, bkc[:, t * 128:(t + 1) * 128]), R(ckv, None, ckv[:, 0, t, :]), identf)
            k.copy("dve", R(kcT, None, kcT[:]), R(bkc, None, bkc[:, 0:256]))
            vc = A("vc", [128, 2, 128], BF16)
            k.copy("dve", R(vc, None, vc[:]), R(ckv, None, ckv[:, 1, :, :]))
        sko = RO["sink"][0]
        nsink = A("nsink", [128, 8], F32)
        k.ts("dve", R(nsink, None, nsink[:]), R(rowb, None, rowb[:, l, sko:sko + 8]), -1.0, ALU.mult)
        P_ = 3
        p_sb = [A("p_sb%d" % i, [128, 640], BF16) for i in range(P_)]
        pT_sb = [A("pT_sb%d" % i, [128, 640], BF16) for i in range(P_)]
        sm = [A("asm%d" % i, [128, 16], F32) for i in range(P_)]
        rall = [A("rall%d" % i, [128, 8], F32) for i in range(2)]
        o_tok = [A("o_tok%d" % i, [128, 512], BF16) for i in range(2)]
        SA = [(self.B[4], self.B[5]), (self.B[6], self.B[7])]
        PT = [self.B[2], self.B[3]]
        OB = [self.B[0], self.B[1]]
        old_banks = self.ada_banks
        self.ada_banks = (2, 4)

        def combo(it, s, qb, h, qi):
            qtok = slice(s * L + qb * 128, s * L + (qb + 1) * 128)
            blocks = []
            if g == 1:
                for t in range(2):
                    blocks.append(("c", t, None))
                for db in (-1, 0, 1):
                    kb = qb + db
                    if 0 <= kb < NB:
                        blocks.append(("b", kb, None if db == 0 else (self.negfb if db == -1 else self.negbb)))
            else:
                for kb in range(NB):
                    blocks.append(("b", s * NB + kb, None))
            nblk = len(blocks)
            gk = h // 4
            jq = h % 4
            pr = slice(gk * 64, (gk + 1) * 64)
            sA, sB = SA[it % 2]
            for bi, (kind, idx, mask) in enumerate(blocks):
                bank_, col = (sA, bi * 128) if bi < 4 else (sB, 0)
                if kind == "c":
                    rk = R(kcT, None, kcT[pr, idx * 128:(idx + 1) * 128])
                else:
                    rk = R(kT, None, kT[pr, idx * 128:(idx + 1) * 128])
                k.mm(R(bank_, None, bank_[:, col:col + 128]), R(qT, None, qT[pr, jq, qtok]), rk,
                     start=True, stop=(mask is None))
                if mask is not None:
                    k.mm(R(bank_, None, bank_[:, col:col + 128]), self.identb, mask, start=False, stop=True)
            yield
            nA = min(nblk, 4) * 128
            m_ = sm[it % P_]
            k.reduce("dve", R(m_, None, m_[:, 0:1]), R(sA, None, sA[:, 0:nA]), ALU.max)
            if nblk > 4:
                k.reduce("dve", R(m_, None, m_[:, 1:2]), R(sB, None, sB[:, 0:128]), ALU.max)
                k.tt("dve", R(m_, None, m_[:, 0:1]), R(m_, None, m_[:, 0:1]), R(m_, None, m_[:, 1:2]), ALU.max)
            sink = R(rowb, None, rowb[:, l, sko + h: sko + h + 1])
            negm = R(m_, None, m_[:, 3:4])
            k.ts("dve", negm, R(m_, None, m_[:, 0:1]), -0.125, ALU.mult, R(nsink, None, nsink[:, h:h + 1]), ALU.min)
            ps_ = p_sb[it % P_]
            k.act(R(ps_, None, ps_[:, 0:nA]), R(sA, None, sA[:, 0:nA]), AF.Exp, bias=negm, scale=0.125,
                  accum_out=R(m_, None, m_[:, 4:5]))
            nsum = 2
            k.act(R(m_, None, m_[:, 5:6]), negm, AF.Exp, bias=sink, scale=1.0)
            if nblk > 4:
                k.act(R(ps_, None, ps_[:, 512:640]), R(sB, None, sB[:, 0:128]), AF.Exp, bias=negm, scale=0.125,
                      accum_out=R(m_, None, m_[:, 6:7]))
                nsum = 3
            k.reduce("dve", R(m_, None, m_[:, 7:8]), R(m_, None, m_[:, 4:4 + nsum]), ALU.add)
            ra = rall[qi % 2]
            k.recip(R(ra, None, ra[:, h:h + 1]), R(m_, None, m_[:, 7:8]))
            yield
            ptb = PT[it % 2]
            pb2 = ptb[:].bitcast(BF16)
            for bi in range(nblk):
                k.tr(R(ptb, None, pb2[:, bi * 128:(bi + 1) * 128]), R(ps_, None, ps_[:, bi * 128:(bi + 1) * 128]), self.identb)
            pt = pT_sb[it % P_]
            k.copy("act" if it % 2 else "dve", R(pt, None, pt[:, 0:nblk * 128]), R(ptb, None, pb2[:, 0:nblk * 128]))
            yield
            Ob = OB[qi % 2]
            for bi, (kind, idx, mask) in enumerate(blocks):
                vv = vc if kind == "c" else v_tok
                k.mm(R(Ob, None, Ob[:, h * 64:(h + 1) * 64]), R(pt, None, pt[:, bi * 128:(bi + 1) * 128]),
                     R(vv, None, vv[:, idx, gk * 64:(gk + 1) * 64]), start=(bi == 0), stop=(bi == nblk - 1))
            if h == 7:
                ot = o_tok[qi % 2]
                k.tt("dve", R(ot, None, ot[:].rearrange("p (h d) -> p h d", d=64)),
                     R(Ob, None, Ob[:].rearrange("p (h d) -> p h d", d=64)),
                     R(ra, None, bc(ra[:, 0:8], [[1, 8], [0, 64]])), ALU.mult)
                tb = PT[(it + 1) % 2]
                tb2 = tb[:].bitcast(BF16)
                for j in range(4):
                    k.tr(R(tb, None, tb2[:, j * 128:(j + 1) * 128]), R(ot, None, ot[:, j * 128:(j + 1) * 128]), self.identb)
                k.copy("act", R(oT, None, oT[:, 0:4, qtok]), R(tb, None, tb2[:, 0:512].rearrange("p (j t) -> p j t", t=128)))
                self.slot(2)

        def combos():
            it = 0
            qi = 0
            for s in range(nseq):
                for qb in range(NB):
                    for h in range(8):
                        yield combo(it, s, qb, h, qi)
                        it += 1
                    qi += 1
        run_pipeline(combos(), P_)
        self.ada_banks = old_banks


from concourse.bass_utils import run_bass_kernel_spmd
import os
DBG_OUT = {}

_NC_CACHE = {}


def build_nc():
    if "nc" in _NC_CACHE:
        return _NC_CACHE["nc"]
    dbg = bool(os.environ.get("KDBG"))
    nc0 = bass.Bass("TRN2", target_bir_lowering=False)
    with ExitStack() as es:
        b0 = Builder(nc0, es, dbg=dbg, plan=None)
        b0.build()
        plan = b0.wplan
    nc = bass.Bass("TRN2", target_bir_lowering=False)
    with ExitStack() as es:
        b = Builder(nc, es, dbg=dbg, plan=plan)
        b.build()
    _NC_CACHE["nc"] = nc
    _NC_CACHE["plan"] = plan
    return nc


def kernel(x_prompt, x_sample, cache_attn_k, cache_attn_v, state_ssm, state_ret, c, c_ctx,
           ada_w, ada_b, norm_mix, norm_mlp, w_in, conv_w, conv_b, conv_ln_g, conv_ln_b,
           ssm_conv_w, ssm_conv_b, ssm_a_log, ssm_dt_bias, ssm_d, ssm_norm,
           ret_log_decay, ret_gn_g, att_sink, w_out, w1, w2, final_norm):
    f = lambda a: np.ascontiguousarray(np.asarray(a, dtype=np.float32))
    x_prompt, x_sample = f(x_prompt), f(x_sample)
    n = 8
    perm = np.arange(5392)
    aq0 = 4624
    newcols = []
    for j in range(4):
        newcols.extend(range(aq0 + j * 64, aq0 + (j + 1) * 64))
        newcols.extend(range(aq0 + (4 + j) * 64, aq0 + (5 + j) * 64))
    perm[aq0:aq0 + 512] = np.array(newcols)
    w_in_p = np.ascontiguousarray(f(w_in)[:, :, perm])
    ada_w_, w_out_, w1_, w2_ = f(ada_w), f(w_out), f(w1), f(w2)
    ada_bT = np.ascontiguousarray(f(ada_b).reshape(NL, 96, 128).transpose(2, 0, 1))
    nmixT = np.ascontiguousarray(f(norm_mix).reshape(NL, 16, 128).transpose(2, 0, 1))
    nmlpT = np.ascontiguousarray(f(norm_mlp).reshape(NL, 16, 128).transpose(2, 0, 1))
    fnT = np.ascontiguousarray(f(final_norm).reshape(16, 128).transpose(1, 0))
    convp = np.zeros((128, NL, 4, 34), np.float32)
    convp[:, :, :, 0:31] = f(conv_w).reshape(NL, 31, 4, 128).transpose(3, 0, 2, 1)
    convp[:, :, :, 31] = f(conv_b).reshape(NL, 4, 128).transpose(2, 0, 1)
    convp[:, :, :, 32] = f(conv_ln_g).reshape(NL, 4, 128).transpose(2, 0, 1)
    convp[:, :, :, 33] = f(conv_ln_b).reshape(NL, 4, 128).transpose(2, 0, 1)
    sconvp = np.zeros((128, NL, 8, 6), np.float32)
    sconvp[:, :, :, 0:5] = f(ssm_conv_w).reshape(NL, 5, 8, 128).transpose(3, 0, 2, 1)
    sconvp[:, :, :, 5] = f(ssm_conv_b).reshape(NL, 8, 128).transpose(2, 0, 1)
    rowp = np.concatenate([f(ssm_a_log).reshape(NL, 16), f(ssm_dt_bias).reshape(NL, 16), f(ssm_d).reshape(NL, 16),
                           f(ret_log_decay).reshape(NL, 8), f(att_sink).reshape(NL, 8), f(ssm_norm).reshape(NL, 512),
                           f(ret_gn_g).reshape(NL, 512)], axis=1)
    rowp = np.ascontiguousarray(rowp)
    cst = make_consts()
    ck, cv = f(cache_attn_k), f(cache_attn_v)
    ss, sr = f(state_ssm), f(state_ret)
    c_, cc = f(c), f(c_ctx)
    in_maps = []
    for i in range(n):
        cvec = np.stack([cc, c_[i]], axis=0)
        in_maps.append(dict(
            xin=np.ascontiguousarray(np.concatenate([x_prompt[2 * i], x_prompt[2 * i + 1], x_sample[i]], axis=0)),
            cT=np.ascontiguousarray(cvec.reshape(2, 16, 128).transpose(2, 1, 0)),
            ada_w=ada_w_, ada_bT=ada_bT, nmixT=nmixT, nmlpT=nmlpT, fnT=fnT,
            w_in=w_in_p, w_out=w_out_, w1=w1_, w2=w2_, convp=convp, sconvp=sconvp, rowp=rowp,
            cache_k=np.ascontiguousarray(ck[i].reshape(NL, 256, 128)),
            cache_v=np.ascontiguousarray(cv[i].reshape(NL, 256, 128)),
            st_ssm=np.ascontiguousarray(ss[i].reshape(NL, 2, 512, 128)),
            st_ret=np.ascontiguousarray(sr[i].reshape(NL, 2, 512, 128)),
            cst=cst,
        ))
    nc = build_nc()
    res = run_bass_kernel_spmd(nc, in_maps, core_ids=list(range(n)))
    rs = res.results
    if os.environ.get("KDBG"):
        DBG_OUT["dbg_o"] = np.asarray(rs[0]["dbg_o"])
        DBG_OUT["dbg_x"] = np.asarray(rs[0]["dbg_x"])
        DBG_OUT["dbg_m"] = np.asarray(rs[0]["dbg_m"])
    y_p = np.zeros((16, 256, DM), np.float32)
    y_s = np.zeros((8, 1024, DM), np.float32)
    nk = np.zeros((16, NL, 256, 2, 64), np.float32)
    nv = np.zeros((16, NL, 256, 2, 64), np.float32)
    nss = np.zeros((16, NL, 2, 8, 64, 128), np.float32)
    nrr = np.zeros((16, NL, 2, 4, 128, 128), np.float32)
    for i in range(n):
        r = rs[i]
        y = np.asarray(r["y"])
        for s in range(2):
            y_p[2 * i + s] = y[s * 256:(s + 1) * 256]
            nk[2 * i + s] = np.asarray(r["newk"])[s].reshape(NL, 256, 2, 64)
            nv[2 * i + s] = np.asarray(r["newv"])[s].reshape(NL, 256, 2, 64)
            nss[2 * i + s] = np.asarray(r["newssm"])[s].reshape(NL, 2, 8, 64, 128)
            nrr[2 * i + s] = np.asarray(r["newret"])[s].reshape(NL, 2, 4, 128, 128)
        y_s[i] = y[512:1536]
    return (y_p, y_s, nk, nv, nss, nrr)
```
